# Optimizing a Trainium2 kernel written in Bass

```python
import jax, jax.numpy as jnp
from jax import lax
import numpy as np

D_MODEL = 2048
BATCH = 2
SEQ = 16384
DEPTH = 4
DEC_BATCH = 1
DEC_SEQ = 16384
PAST_LEN = 128

N_MIXERS = 2
N_ATTN_LAYERS = (DEPTH + N_MIXERS - 1) // N_MIXERS
N_CONV_LAYERS = DEPTH // N_MIXERS
HEAD_DIM = 128
DILATION_GROUPS = ((128, 1), (512, 4), (2048, 16))
N_GROUPS = len(DILATION_GROUPS)
HEADS_PER_GROUP = 4
N_HEADS = N_GROUPS * HEADS_PER_GROUP
ATTN_WIDTH = N_HEADS * HEAD_DIM
ROT_DIM = HEAD_DIM // 4
ROPE_THETA = 500000.0
CONV_CHANNELS = D_MODEL
CONV_WIDTH = 31
D_FF = 5632
FFN_RESIDUAL_WEIGHT = 0.5
RMS_EPS = 1e-6
LN_EPS = 1e-5
NEG_INF = -1e30

kernel_name = "hybrid_dilated_attn_conformer_conv_encoder"


def _rms_norm(x, g):
    x32 = x.astype(jnp.float32)
    y = x32 * lax.rsqrt(jnp.mean(x32 * x32, axis=-1, keepdims=True) + RMS_EPS)
    return (y * g.astype(jnp.float32)).astype(x.dtype)


def _swiglu(x, w_gate, w_up, w_down):
    return (jax.nn.silu(x @ w_gate) * (x @ w_up)) @ w_down


def _partial_rope(t, seq_len):
    pos = jnp.arange(seq_len, dtype=jnp.float32)
    freqs = ROPE_THETA ** (-jnp.arange(0, ROT_DIM, 2, dtype=jnp.float32) / ROT_DIM)
    ang = pos[:, None] * freqs[None, :]
    cos = jnp.cos(ang)[None, :, None, :]
    sin = jnp.sin(ang)[None, :, None, :]
    t32 = t.astype(jnp.float32)
    half = ROT_DIM // 2
    t1, t2, rest = t32[..., :half], t32[..., half:ROT_DIM], t32[..., ROT_DIM:]
    out = jnp.concatenate([t1 * cos - t2 * sin, t2 * cos + t1 * sin, rest], axis=-1)
    return out.astype(t.dtype)


def _local_attention(q, k, v, half):
    N, L, H, Dh = q.shape
    nb = -(-L // half)
    Lp = nb * half
    pad = Lp - L
    qb = jnp.pad(q, ((0, 0), (0, pad), (0, 0), (0, 0))).reshape(N, nb, half, H, Dh).astype(jnp.float32)

    def key_blocks(t):
        tp = jnp.pad(t, ((0, 0), (half, pad + half), (0, 0), (0, 0))).reshape(N, nb + 2, half, H, Dh)
        return jnp.concatenate([tp[:, :-2], tp[:, 1:-1], tp[:, 2:]], axis=2).astype(jnp.float32)

    kb, vb = key_blocks(k), key_blocks(v)
    scores = jnp.einsum('nbqhd,nbkhd->nbhqk', qb, kb) * (Dh ** -0.5)
    blk = jnp.arange(nb, dtype=jnp.int32)[:, None] * half
    qpos = blk + jnp.arange(half, dtype=jnp.int32)[None, :]
    kpos = blk - half + jnp.arange(3 * half, dtype=jnp.int32)[None, :]
    valid = ((jnp.abs(qpos[:, :, None] - kpos[:, None, :]) <= half)
             & (kpos[:, None, :] >= 0) & (kpos[:, None, :] < L))
    scores = jnp.where(valid[None, :, None], scores, NEG_INF)
    m = jnp.max(scores, axis=-1, keepdims=True)
    e = jnp.exp(scores - m)
    den = jnp.sum(e, axis=-1, keepdims=True)
    out = jnp.einsum('nbhqk,nbkhd->nbqhd', e / den, vb)
    lse = (m + jnp.log(den))[..., 0]
    out = out.reshape(N, Lp, H, Dh)[:, :L]
    lse = lse.transpose(0, 1, 3, 2).reshape(N, Lp, H)[:, :L]
    return out, lse


def _dilated_attention(q, k, v, dilation, half):
    B, S, H, Dh = q.shape
    L = S // dilation

    def to_residue(t):
        return t.reshape(B, L, dilation, H, Dh).transpose(0, 2, 1, 3, 4).reshape(B * dilation, L, H, Dh)

    out, lse = _local_attention(to_residue(q), to_residue(k), to_residue(v), half)
    out = out.reshape(B, dilation, L, H, Dh).transpose(0, 2, 1, 3, 4).reshape(B, S, H, Dh)
    lse = lse.reshape(B, dilation, L, H).transpose(0, 2, 1, 3).reshape(B, S, H)
    return out, lse


def _dilated_mixture_attention(x, w_qkv, w_o):
    B, S, _ = x.shape
    qkv = (x @ w_qkv).reshape(B, S, 3, N_HEADS, HEAD_DIM)
    q = _partial_rope(qkv[:, :, 0], S)
    k = _partial_rope(qkv[:, :, 1], S)
    v = qkv[:, :, 2]
    outs, lses = [], []
    for g, (window, dilation) in enumerate(DILATION_GROUPS):
        sl = slice(g * HEADS_PER_GROUP, (g + 1) * HEADS_PER_GROUP)
        o, l = _dilated_attention(q[:, :, sl], k[:, :, sl], v[:, :, sl], dilation, window // (2 * dilation))
        outs.append(o)
        lses.append(l)
    o = jnp.stack(outs, axis=2)
    lam = jax.nn.softmax(jnp.stack(lses, axis=2), axis=2)
    o = (o * lam[..., None]).reshape(B, S, ATTN_WIDTH).astype(x.dtype)
    return o @ w_o


def _conformer_conv(x, w_in, b_in, w_dw, b_dw, ln_g, ln_b, w_out, b_out):
    h = x @ w_in + b_in
    h = h[..., :CONV_CHANNELS] * jax.nn.sigmoid(h[..., CONV_CHANNELS:])
    h = lax.conv_general_dilated(
        h, w_dw[:, None, :].astype(h.dtype), window_strides=(1,),
        padding=[(CONV_WIDTH // 2, CONV_WIDTH // 2)],
        dimension_numbers=('NWC', 'WIO', 'NWC'),
        feature_group_count=CONV_CHANNELS) + b_dw
    h32 = h.astype(jnp.float32)
    mu = jnp.mean(h32, axis=-1, keepdims=True)
    var = jnp.mean(jnp.square(h32 - mu), axis=-1, keepdims=True)
    h32 = (h32 - mu) * lax.rsqrt(var + LN_EPS) * ln_g.astype(jnp.float32) + ln_b.astype(jnp.float32)
    h = jax.nn.silu(h32).astype(x.dtype)
    return h @ w_out + b_out


def _trunk(x, ffn_norm, ffn_w_gate, ffn_w_up, ffn_w_down, mix_norm,
           attn_w_qkv, attn_w_o, conv_w_in, conv_b_in, conv_w_dw, conv_b_dw,
           conv_ln_g, conv_ln_b, conv_w_out, conv_b_out, final_norm):
    for i in range(DEPTH):
        x = x + FFN_RESIDUAL_WEIGHT * _swiglu(_rms_norm(x, ffn_norm[i, 0]), ffn_w_gate[i, 0], ffn_w_up[i, 0], ffn_w_down[i, 0])
        h = _rms_norm(x, mix_norm[i])
        j = i // N_MIXERS
        if i % N_MIXERS == 0:
            h = _dilated_mixture_attention(h, attn_w_qkv[j], attn_w_o[j])
        else:
            h = _conformer_conv(h, conv_w_in[j], conv_b_in[j], conv_w_dw[j], conv_b_dw[j],
                                conv_ln_g[j], conv_ln_b[j], conv_w_out[j], conv_b_out[j])
        x = x + h
        x = x + FFN_RESIDUAL_WEIGHT * _swiglu(_rms_norm(x, ffn_norm[i, 1]), ffn_w_gate[i, 1], ffn_w_up[i, 1], ffn_w_down[i, 1])
    return _rms_norm(x, final_norm)


def setup_inputs(seed: int = 0) -> dict:
    key = jax.random.key(seed)
    ks = jax.random.split(key, 20)
    f32 = jnp.float32

    def nrm(k, shape, scale):
        return jax.random.normal(k, shape, f32) * scale

    return {
        "x_prompt": nrm(ks[0], (BATCH, SEQ, D_MODEL), 1.0),
        "x_sample": nrm(ks[1], (DEC_BATCH, DEC_SEQ, D_MODEL), 1.0),
        "ffn_norm": 1.0 + nrm(ks[2], (DEPTH, 2, D_MODEL), 0.02),
        "ffn_w_gate": nrm(ks[3], (DEPTH, 2, D_MODEL, D_FF), D_MODEL ** -0.5),
        "ffn_w_up": nrm(ks[4], (DEPTH, 2, D_MODEL, D_FF), D_MODEL ** -0.5),
        "ffn_w_down": nrm(ks[5], (DEPTH, 2, D_FF, D_MODEL), D_FF ** -0.5),
        "mix_norm": 1.0 + nrm(ks[6], (DEPTH, D_MODEL), 0.02),
        "attn_w_qkv": nrm(ks[7], (N_ATTN_LAYERS, D_MODEL, 3 * ATTN_WIDTH), D_MODEL ** -0.5),
        "attn_w_o": nrm(ks[8], (N_ATTN_LAYERS, ATTN_WIDTH, D_MODEL), ATTN_WIDTH ** -0.5),
        "conv_w_in": nrm(ks[9], (N_CONV_LAYERS, D_MODEL, 2 * CONV_CHANNELS), D_MODEL ** -0.5),
        "conv_b_in": nrm(ks[10], (N_CONV_LAYERS, 2 * CONV_CHANNELS), 0.02),
        "conv_w_dw": nrm(ks[11], (N_CONV_LAYERS, CONV_WIDTH, CONV_CHANNELS), CONV_WIDTH ** -0.5),
        "conv_b_dw": nrm(ks[12], (N_CONV_LAYERS, CONV_CHANNELS), 0.02),
        "conv_ln_g": 1.0 + nrm(ks[13], (N_CONV_LAYERS, CONV_CHANNELS), 0.02),
        "conv_ln_b": nrm(ks[14], (N_CONV_LAYERS, CONV_CHANNELS), 0.02),
        "conv_w_out": nrm(ks[15], (N_CONV_LAYERS, CONV_CHANNELS, D_MODEL), CONV_CHANNELS ** -0.5),
        "conv_b_out": nrm(ks[16], (N_CONV_LAYERS, D_MODEL), 0.02),
        "final_norm": 1.0 + nrm(ks[17], (D_MODEL,), 0.02),
    }


def reference(x_prompt, x_sample, ffn_norm, ffn_w_gate, ffn_w_up, ffn_w_down, mix_norm,
              attn_w_qkv, attn_w_o, conv_w_in, conv_b_in, conv_w_dw, conv_b_dw,
              conv_ln_g, conv_ln_b, conv_w_out, conv_b_out, final_norm):
    y_prompt = _trunk(x_prompt, ffn_norm, ffn_w_gate, ffn_w_up, ffn_w_down, mix_norm,
                      attn_w_qkv, attn_w_o, conv_w_in, conv_b_in, conv_w_dw, conv_b_dw,
                      conv_ln_g, conv_ln_b, conv_w_out, conv_b_out, final_norm)
    y_sample = _trunk(x_sample, ffn_norm, ffn_w_gate, ffn_w_up, ffn_w_down, mix_norm,
                      attn_w_qkv, attn_w_o, conv_w_in, conv_b_in, conv_w_dw, conv_b_dw,
                      conv_ln_g, conv_ln_b, conv_w_out, conv_b_out, final_norm)
    return (y_prompt, y_sample)
```

```python
import math
from contextlib import ExitStack

import numpy as np
import concourse.bass as bass
import concourse.mybir as mybir
from concourse.bass_utils import run_bass_kernel_spmd

F32 = mybir.dt.float32
BF16 = mybir.dt.bfloat16
AF = mybir.ActivationFunctionType
ALU = mybir.AluOpType

P = 128
D = 2048
KD = 16
DFF = 5632
KF = 44
TS = 512
NHEAD = 12
AW = 1536
CW = 31
DEPTH = 4
SEQ = 16384
NSEQ = 3
NCORES = 8
OWN = 6144
HALO = 2560
NT = (OWN + 2 * HALO) // TS
LL = NT * TS
OWN_T0 = HALO // TS
OWN_T1 = OWN_T0 + OWN // TS
BIG = 2048.0
RMS_EPS = 1e-6
LN_EPS = 1e-5
DIL = (1, 4, 16)
ENGS = ("pe", "act", "dve", "pool", "sp")


class Op:
    __slots__ = ("eng", "fn", "waits", "is_dma", "sem", "semval", "need_sig", "sig")

    def __init__(self, eng, fn, is_dma=False):
        self.eng = eng
        self.fn = fn
        self.waits = []
        self.is_dma = is_dma
        self.sem = None
        self.semval = 0
        self.need_sig = False
        self.sig = 0


class Sched:
    def __init__(self, nc, stack):
        self.nc = nc
        self.esem = {e: stack.enter_context(nc.semaphore("sg_" + e)) for e in ENGS}
        self.ecount = {e: 0 for e in ENGS}
        self.dsem = {}
        self.dcount = {}
        self.stack = stack
        self.begin()

    def dma_sem(self, name):
        if name not in self.dsem:
            self.dsem[name] = self.stack.enter_context(self.nc.semaphore("dm_" + name))
            self.dcount[name] = 0
        return name

    def begin(self):
        self.q = {e: [] for e in ENGS}
        self.res = {}

    def _deps(self, o, reads, writes):
        res = self.res
        deps = []
        for k in reads:
            st = res.get(k)
            if st is not None and st[0] is not None:
                deps.append((st[0], "raw"))
        for k in writes:
            st = res.get(k)
            if st is not None:
                if st[0] is not None:
                    deps.append((st[0], "waw"))
                for r in st[1]:
                    deps.append((r, "war"))
        for k in reads:
            st = res.get(k)
            if st is None:
                res[k] = [None, [o]]
            else:
                st[1].append(o)
        for k in writes:
            res[k] = [o, []]
        for d, kind in deps:
            if d is o:
                continue
            if not d.is_dma and d.eng == o.eng and not o.is_dma:
                if o.eng == "pe":
                    continue
                if kind != "raw":
                    continue
            o.waits.append(d)
            if not d.is_dma:
                d.need_sig = True

    def op(self, eng, fn, reads=(), writes=()):
        o = Op(eng, fn)
        self._deps(o, reads, writes)
        self.q[eng].append(o)
        return o

    def dma(self, eng, out, in_, reads=(), writes=(), sem=None):
        name = self.dma_sem(sem)
        o = Op(eng, (lambda e, out=out, in_=in_: e.dma_start(out=out, in_=in_)), is_dma=True)
        self._deps(o, reads, writes)
        self.dcount[name] += 16
        o.sem = name
        o.semval = self.dcount[name]
        self.q[eng].append(o)
        return o

    def emit(self, final=False):
        nc = self.nc
        for e in ENGS:
            lastc = None
            for o in self.q[e]:
                if not o.is_dma:
                    lastc = o
            if lastc is not None:
                lastc.need_sig = True
            c = self.ecount[e]
            for o in self.q[e]:
                if not o.is_dma and o.need_sig:
                    c += 1
                    o.sig = c
            self.ecount[e] = c
        end_e = dict(self.ecount)
        end_d = dict(self.dcount)
        esem, dsem = self.esem, self.dsem

        def replay(e, eng):
            waited_e = {}
            waited_d = {}
            for o in self.q[e]:
                for d in o.waits:
                    if d.is_dma:
                        if waited_d.get(d.sem, 0) < d.semval:
                            eng.wait_ge(dsem[d.sem], d.semval)
                            waited_d[d.sem] = d.semval
                    else:
                        if waited_e.get(d.eng, 0) < d.sig:
                            eng.wait_ge(esem[d.eng], d.sig)
                            waited_e[d.eng] = d.sig
                ins = o.fn(eng)
                if o.is_dma:
                    ins.then_inc(dsem[o.sem], 16)
                elif o.need_sig:
                    ins.then_inc(esem[e], 1)
            for f in ENGS:
                if f != e and end_e[f] > 0 and waited_e.get(f, 0) < end_e[f]:
                    eng.wait_ge(esem[f], end_e[f])
            for name, v in end_d.items():
                if v > 0 and waited_d.get(name, 0) < v:
                    eng.wait_ge(dsem[name], v)

        with nc.Block() as block:
            @block.tensor
            def _(eng):
                replay("pe", eng)

            @block.scalar
            def _(eng):
                replay("act", eng)

            @block.vector
            def _(eng):
                replay("dve", eng)

            @block.gpsimd
            def _(eng):
                replay("pool", eng)

            @block.sync
            def _(eng):
                replay("sp", eng)
        self.begin()


class Cfg:
    def __init__(self, **kw):
        self.nt = NT
        self.own = (OWN_T0, OWN_T1)
        self.depth = DEPTH
        self.phases = None
        self.dbg = ()
        for k, v in kw.items():
            setattr(self, k, v)


def phase_plan():
    pl = []
    r0 = (0, 22)
    r1 = (2, 20)
    r2 = (4, 18)
    r3 = (5, 17)
    pl += [("ffn", 0, 0, r0, "xin"), ("qkv", 0, 0, r0), ("att", 0, 0, r1), ("wo", 0, 0, r1), ("ffn", 0, 1, r1)]
    pl += [("ffn", 1, 0, r1), ("c1", 1, 0, (1, 21)), ("c2", 1, 0, r1), ("ffn", 1, 1, r1)]
    pl += [("ffn", 2, 0, r1), ("qkv", 2, 1, r1), ("att", 2, 1, r2), ("wo", 2, 1, r2), ("ffn", 2, 1, r2)]
    pl += [("ffn", 3, 0, r2), ("c1", 3, 1, r2), ("c2", 3, 1, r3), ("ffn", 3, 1, r3)]
    pl += [("final", 0, 0, r3)]
    return pl


def att_tiles(rng):
    a, b = rng
    offs = list(range(a, b - 3, 4))
    if not offs or offs[-1] + 4 < b:
        offs.append(b - 4)
    return offs


class NcProxy:
    def __init__(self, nc):
        self._nc = nc
        self.uid = 0

    def __getattr__(self, name):
        return getattr(self._nc, name)

    def sbuf_tensor(self, name, shape, dt):
        return self._nc.sbuf_tensor(f"{name}_u{self.uid}", shape, dt)

    def psum_tensor(self, name, shape, dt):
        return self._nc.psum_tensor(f"{name}_u{self.uid}", shape, dt)


class Builder:
    def __init__(self, cfg):
        self.cfg = cfg
        self.nc_real = bass.Bass("TRN2", target_bir_lowering=False)
        self.nc = NcProxy(self.nc_real)
        self.top = ExitStack()

    def declare(self):
        nc = self.nc
        c = self.cfg
        ll = c.nt * TS
        self.ll = ll

        def inp(name, shape, dt=F32):
            return nc.dram_tensor(name, list(shape), dt, kind="ExternalInput").ap()

        def scr(name, shape, dt):
            kind = "ExternalOutput" if name in c.dbg else "Internal"
            return nc.dram_tensor(name, list(shape), dt, kind=kind).ap()

        self.xin = inp("xin", (D, ll))
        self.wg = inp("wg", (DEPTH * 2, KF, P, KD * P))
        self.wu = inp("wu", (DEPTH * 2, KF, P, KD * P))
        self.wd = inp("wd", (DEPTH * 2, KD, P, KF * P))
        self.wqk = inp("wqk", (2, 24, P, KD * P))
        self.wv = inp("wv", (2, 3, P, KD * 512))
        self.wo = inp("wo", (2, KD, P, NHEAD * P))
        self.win = inp("win", (2, 32, P, KD * P))
        self.wout = inp("wout", (2, KD, P, KD * P))
        self.vecs = inp("vecs", (P, self.n_vec_cols()))
        self.cosT = inp("cosT", (32, ll))
        self.sinT = inp("sinT", (32, ll))
        self.ohk = inp("ohk", (3, ll))
        self.ohq = inp("ohq", (3, ll))
        self.flags = inp("flags", (P, c.nt * 2))
        self.consts = inp("consts", (P, 3 * P + 256))
        self.xs = scr("xs", (D, ll), F32)
        self.qs = scr("qs", (AW, ll), BF16)
        self.ks = scr("ks", (AW, ll), BF16)
        self.vs = scr("vs", (ll, AW), BF16)
        self.os_ = scr("os", (AW, ll), BF16)
        self.gs = scr("gs", (D, ll), F32)
        o0, o1 = c.own
        self.yout = nc.dram_tensor("yT", [D, (o1 - o0) * TS], F32, kind="ExternalOutput").ap()

    VEC = {}

    @classmethod
    def n_vec_cols(cls):
        if not cls.VEC:
            col = 0

            def add(name, n):
                nonlocal col
                cls.VEC[name] = col
                col += n
            for l in range(DEPTH):
                for f in range(2):
                    add(("ffn_norm", l, f), KD)
                add(("mix_norm", l), KD)
            for j in range(2):
                add(("b_in", j), 32)
                add(("w_dw", j), KD * CW)
                add(("b_dw", j), KD)
                add(("ln_g", j), KD)
                add(("ln_b", j), KD)
                add(("b_out", j), KD)
            add(("final_norm",), KD)
            cls.VEC["_n"] = col
        return cls.VEC["_n"]

    def xview(self, t):
        return t.rearrange("(k p) t -> p k t", p=P)

    def build(self):
        nc = self.nc
        c = self.cfg
        self.declare()
        top = self.top
        with top:
            S = self.S = Sched(nc, top)
            self.vec_sb = top.enter_context(nc.sbuf_tensor("vec_sb", [P, self.n_vec_cols()], F32))
            self.ident = top.enter_context(nc.sbuf_tensor("ident", [P, P], BF16))
            self.ones = top.enter_context(nc.sbuf_tensor("ones", [P, P], BF16))
            self.perm = top.enter_context(nc.sbuf_tensor("perm", [P, P], BF16))
            self.maskb = top.enter_context(nc.sbuf_tensor("maskb", [P, 256], BF16))
            self.flag_sb = top.enter_context(nc.sbuf_tensor("flag_sb", [P, c.nt * 2], F32))
            self.epsc = top.enter_context(nc.sbuf_tensor("epsc", [P, 2], F32))
            S.op("pool", (lambda e: e.memset(self.epsc[:, 0:1], RMS_EPS)), writes=["epsc"])
            S.op("pool", (lambda e: e.memset(self.epsc[:, 1:2], LN_EPS)), writes=["epsc"])
            S.dma("sp", self.vec_sb[:], self.vecs[:, :], writes=["vec"], sem="c0")
            S.dma("sp", self.flag_sb[:], self.flags[:, :], writes=["flag"], sem="c0")
            S.dma("pool", self.ident[:], self.consts[:, 0:P], writes=["ident"], sem="c1")
            S.dma("pool", self.ones[:], self.consts[:, P:2 * P], writes=["ones"], sem="c1")
            S.dma("pool", self.perm[:], self.consts[:, 2 * P:3 * P], writes=["perm"], sem="c1")
            S.dma("pool", self.maskb[:], self.consts[:, 3 * P:3 * P + 256], writes=["maskb"], sem="c1")
            S.emit()
            plan = c.phases if c.phases is not None else phase_plan()
            for ph in plan:
                kind = ph[0]
                self.nc.uid += 1
                if kind == "ffn":
                    src = self.xin if (len(ph) > 4 and ph[4] == "xin") else self.xs
                    self.ph_ffn(ph[1], ph[2], ph[3], src)
                elif kind == "qkv":
                    self.ph_qkv(ph[1], ph[2], ph[3])
                elif kind == "att":
                    self.ph_att(ph[3])
                elif kind == "wo":
                    self.ph_wo(ph[2], ph[3])
                elif kind == "c1":
                    self.ph_c1(ph[1], ph[2], ph[3])
                elif kind == "c2":
                    self.ph_c2(ph[2], ph[3])
                elif kind == "final":
                    self.ph_final(ph[3])
                elif kind == "copy":
                    self.ph_copy(ph[3])
                else:
                    raise ValueError(kind)
        return self.nc_real

    def vcol(self, key, k=0, n=1):
        c0 = self.VEC[key] + k
        return self.vec_sb[:, c0:c0 + n]

    def emit_norm(self, st, src, t0, ns, gkey, xn, ps_stat, tag):
        nc, S = self.nc, self.S
        srcv = self.xview(src)
        XG = 2
        NXG = 2
        xg = [st.enter_context(nc.sbuf_tensor(f"{tag}_xg{i}", [P, XG, TS], F32)) for i in range(NXG)]
        sq = [st.enter_context(nc.sbuf_tensor(f"{tag}_sq{i}", [P, TS], BF16)) for i in range(2)]
        rstd = [st.enter_context(nc.sbuf_tensor(f"{tag}_rstd{i}", [P, TS], F32)) for i in range(ns)]
        self._norm_bufs = (xg, sq, rstd)

        def run(t0, gkey, src=src):
            srcv = self.xview(src)
            cnt = [0]
            for s in range(ns):
                ts0 = t0 + s * TS
                for kg in range(KD // XG):
                    b = cnt[0] % NXG
                    cnt[0] += 1
                    S.dma("sp", xg[b][:], srcv[:, kg * XG:(kg + 1) * XG, ts0:ts0 + TS],
                          writes=[(tag, "xg", b)], sem=f"{tag}xg{b}")
                    for kk in range(XG):
                        k = kg * XG + kk
                        q = k % 2
                        S.op("act", (lambda e, o=sq[q], i=xg[b], kk=kk:
                                     e.activation(out=o[:], in_=i[:, kk, :], func=AF.Square)),
                             reads=[(tag, "xg", b)], writes=[(tag, "sq", q)])
                        S.op("pe", (lambda e, o=ps_stat, i=sq[q], k=k:
                                    e.matmul(o[:], self.ones[:], i[:], start=(k == 0), stop=(k == KD - 1))),
                             reads=[(tag, "sq", q), "ones"], writes=[(tag, "pstat")])
                S.op("act", (lambda e, o=rstd[s], i=ps_stat:
                             e.activation(out=o[:], in_=i[:], func=AF.Sqrt, bias=self.epsc[:, 0:1], scale=1.0 / D)),
                     reads=[(tag, "pstat"), "epsc"], writes=[(tag, "rstd", s)])
                S.op("dve", (lambda e, o=rstd[s]: e.reciprocal(o[:], o[:])),
                     reads=[(tag, "rstd", s)], writes=[(tag, "rstd", s)])
                for kg in range(KD // XG):
                    b = cnt[0] % NXG
                    cnt[0] += 1
                    S.dma("sp", xg[b][:], srcv[:, kg * XG:(kg + 1) * XG, ts0:ts0 + TS],
                          writes=[(tag, "xg", b)], sem=f"{tag}xg{b}")
                    for kk in range(XG):
                        k = kg * XG + kk
                        S.op("dve", (lambda e, o=xn[s], i=xg[b], kk=kk, k=k, r=rstd[s]:
                                     e.scalar_tensor_tensor(out=o[:, k, :], in0=i[:, kk, :],
                                                            scalar=self.vcol(gkey, k), in1=r[:],
                                                            op0=ALU.mult, op1=ALU.mult)),
                             reads=[(tag, "xg", b), (tag, "rstd", s), "vec"], writes=[(tag, "xn", s, k)])
        return run

    def wload(self, wbuf, slot, src_ap, key, sem):
        self.S.dma("pool", wbuf[slot][:].rearrange("p k c -> p (k c)"), src_ap,
                   writes=[(key, slot)], sem=f"{sem}{slot}")

    def ph_ffn(self, l, f, rng, src):
        nc, S = self.nc, self.S
        lf = l * 2 + f
        a, b = rng
        assert (b - a) % 2 == 0
        with ExitStack() as st:
            xn = [st.enter_context(nc.sbuf_tensor(f"f_xn{s}", [P, KD, TS], BF16)) for s in range(2)]
            h = [st.enter_context(nc.sbuf_tensor(f"f_h{s}", [P, KF, TS], BF16)) for s in range(2)]
            NWG = 3
            wgb = [st.enter_context(nc.sbuf_tensor(f"f_wg{i}", [P, KD, P], BF16)) for i in range(NWG)]
            wub = [st.enter_context(nc.sbuf_tensor(f"f_wu{i}", [P, KD, P], BF16)) for i in range(NWG)]
            wdb = [st.enter_context(nc.sbuf_tensor(f"f_wd{i}", [P, KF, P], BF16)) for i in range(2)]
            sg = [st.enter_context(nc.sbuf_tensor(f"f_sg{i}", [P, TS], BF16)) for i in range(2)]
            xr = [st.enter_context(nc.sbuf_tensor(f"f_xr{i}", [P, TS], F32)) for i in range(2)]
            yo = [st.enter_context(nc.sbuf_tensor(f"f_yo{i}", [P, TS], F32)) for i in range(2)]
            ps_stat = st.enter_context(nc.psum_tensor("f_pstat", [P, TS], F32))
            pg = [st.enter_context(nc.psum_tensor(f"f_pg{i}", [P, TS], F32)) for i in range(2)]
            pu = [st.enter_context(nc.psum_tensor(f"f_pu{i}", [P, TS], F32)) for i in range(2)]
            pd = [st.enter_context(nc.psum_tensor(f"f_pd{i}", [P, TS], F32)) for i in range(2)]
            norm = self.emit_norm(st, src, 0, 2, None, xn, ps_stat, "fn")
            srcv = self.xview(src)
            dstv = self.xview(self.xs)
            wcnt = 0
            dcnt = 0
            rcnt = 0
            for pp in range((b - a) // 2):
                t0 = (a + 2 * pp) * TS
                norm(t0, ("ffn_norm", l, f))
                for j in range(KF):
                    slot = wcnt % NWG
                    wcnt += 1
                    self.wload(wgb, slot, self.wg[lf, j], "wg", "fwg")
                    self.wload(wub, slot, self.wu[lf, j], "wu", "fwu")
                    for s in range(2):
                        for k in range(KD):
                            S.op("pe", (lambda e, o=pg[s], w=wgb[slot], x=xn[s], k=k:
                                        e.matmul(o[:], w[:, k, :], x[:, k, :], start=(k == 0), stop=(k == KD - 1))),
                                 reads=[("wg", slot), ("fn", "xn", s, k)], writes=[("pg", s)])
                        for k in range(KD):
                            S.op("pe", (lambda e, o=pu[s], w=wub[slot], x=xn[s], k=k:
                                        e.matmul(o[:], w[:, k, :], x[:, k, :], start=(k == 0), stop=(k == KD - 1))),
                                 reads=[("wu", slot), ("fn", "xn", s, k)], writes=[("pu", s)])
                        S.op("act", (lambda e, o=sg[s], i=pg[s]:
                                     e.activation(out=o[:], in_=i[:], func=AF.Silu)),
                             reads=[("pg", s)], writes=[("sg", s)])
                        S.op("dve", (lambda e, o=h[s], a_=sg[s], b_=pu[s], j=j:
                                     e.tensor_tensor(o[:, j, :], a_[:], b_[:], ALU.mult)),
                             reads=[("sg", s), ("pu", s)], writes=[("h", s, j)])
                for o_ in range(KD):
                    slot = dcnt % 2
                    dcnt += 1
                    self.wload(wdb, slot, self.wd[lf, o_], "wd", "fwd")
                    for s in range(2):
                        ts0 = t0 + s * TS
                        rb = rcnt % 2
                        rcnt += 1
                        S.dma("sp", xr[rb][:], srcv[:, o_, ts0:ts0 + TS], writes=[("xr", rb)], sem=f"fxr{rb}")
                        for j in range(KF):
                            S.op("pe", (lambda e, o=pd[s], w=wdb[slot], x=h[s], j=j:
                                        e.matmul(o[:], w[:, j, :], x[:, j, :], start=(j == 0), stop=(j == KF - 1))),
                                 reads=[("wd", slot), ("h", s, j)], writes=[("pd", s)])
                        S.op("dve", (lambda e, o=yo[rb], i=pd[s], x=xr[rb]:
                                     e.scalar_tensor_tensor(out=o[:], in0=i[:], scalar=0.5, in1=x[:],
                                                            op0=ALU.mult, op1=ALU.add)),
                             reads=[("pd", s), ("xr", rb)], writes=[("yo", rb)])
                        S.dma("sp", dstv[:, o_, ts0:ts0 + TS], yo[rb][:], reads=[("yo", rb)], sem=f"fyo{rb}")
            S.emit()

    def ph_qkv(self, l, j, rng):
        nc, S = self.nc, self.S
        a, b = rng
        assert (b - a) % 2 == 0
        with ExitStack() as st:
            xn = [st.enter_context(nc.sbuf_tensor(f"q_xn{s}", [P, KD, TS], BF16)) for s in range(2)]
            NW = 3
            wb = [st.enter_context(nc.sbuf_tensor(f"q_w{i}", [P, KD, P], BF16)) for i in range(NW)]
            wvb = [st.enter_context(nc.sbuf_tensor(f"q_wv{i}", [P, KD, 512], BF16)) for i in range(2)]
            cs = st.enter_context(nc.sbuf_tensor("q_cos", [32, 2 * TS], F32))
            sn = st.enter_context(nc.sbuf_tensor("q_sin", [32, 2 * TS], F32))
            NQ = 3
            qb = [st.enter_context(nc.sbuf_tensor(f"q_qb{i}", [P, TS], BF16)) for i in range(NQ)]
            t1 = [st.enter_context(nc.sbuf_tensor(f"q_t1{i}", [32, TS], F32)) for i in range(2)]
            t2 = [st.enter_context(nc.sbuf_tensor(f"q_t2{i}", [32, TS], F32)) for i in range(2)]
            vb = [st.enter_context(nc.sbuf_tensor(f"q_vb{i}", [P, 512], BF16)) for i in range(3)]
            ps_stat = st.enter_context(nc.psum_tensor("q_pstat", [P, TS], F32))
            pq = [st.enter_context(nc.psum_tensor(f"q_pq{i}", [P, TS], F32)) for i in range(2)]
            pp = [st.enter_context(nc.psum_tensor(f"q_pp{i}", [P, TS], F32)) for i in range(2)]
            pv = [st.enter_context(nc.psum_tensor(f"q_pv{i}", [P, 512], F32)) for i in range(2)]
            norm = self.emit_norm(st, self.xs, 0, 2, None, xn, ps_stat, "qn")
            wcnt = 0
            qcnt = 0
            vcnt = 0
            wvcnt = 0
            for pi in range((b - a) // 2):
                t0 = (a + 2 * pi) * TS
                S.dma("sp", cs[:], self.cosT[:, t0:t0 + 2 * TS], writes=["cos"], sem="qcs")
                S.dma("sp", sn[:], self.sinT[:, t0:t0 + 2 * TS], writes=["sin"], sem="qcs")
                norm(t0, ("mix_norm", l))
                for c in range(24):
                    slot = wcnt % NW
                    wcnt += 1
                    self.wload(wb, slot, self.wqk[j, c], "w", "qw")
                    dst = self.qs if c < 12 else self.ks
                    hd = c % 12
                    for s in range(2):
                        ts0 = t0 + s * TS
                        for k in range(KD):
                            S.op("pe", (lambda e, o=pq[s], w=wb[slot], x=xn[s], k=k:
                                        e.matmul(o[:], w[:, k, :], x[:, k, :], start=(k == 0), stop=(k == KD - 1))),
                                 reads=[("w", slot), ("qn", "xn", s, k)], writes=[("pq", s)])
                        qi = qcnt % NQ
                        qcnt += 1
                        ti = qcnt % 2
                        S.op("act", (lambda e, o=qb[qi], i=pq[s]: e.copy(o[:], i[:])),
                             reads=[("pq", s)], writes=[("qb", qi)])
                        S.op("pe", (lambda e, o=pp[s], i=qb[qi]:
                                    e.matmul(o[0:32, :], self.perm[0:32, 0:32], i[0:32, :], start=True, stop=True)),
                             reads=[("qb", qi), "perm"], writes=[("pp", s)])
                        S.op("dve", (lambda e, o=t1[ti], i=pp[s], s=s:
                                     e.tensor_tensor(o[:], i[0:32, :], sn[:, s * TS:(s + 1) * TS], ALU.mult)),
                             reads=[("pp", s), "sin"], writes=[("t1", ti)])
                        S.op("dve", (lambda e, o=t2[ti], i=pq[s], s=s:
                                     e.tensor_tensor(o[:], i[0:32, :], cs[:, s * TS:(s + 1) * TS], ALU.mult)),
                             reads=[("pq", s), "cos"], writes=[("t2", ti)])
                        S.op("pool", (lambda e, o=qb[qi], x=t1[ti], y=t2[ti]:
                                      e.tensor_tensor(o[0:32, :], x[:], y[:], ALU.add)),
                             reads=[("t1", ti), ("t2", ti), ("qb", qi)], writes=[("qb", qi)])
                        S.dma("sp", dst[hd * P:(hd + 1) * P, ts0:ts0 + TS], qb[qi][:], reads=[("qb", qi)], sem=f"qst{qi}")
                for g in range(3):
                    slot = wvcnt % 2
                    wvcnt += 1
                    S.dma("pool", wvb[slot][:].rearrange("p k c -> p (k c)"), self.wv[j, g],
                          writes=[("wv", slot)], sem=f"qwv{slot}")
                    for s in range(2):
                        for tb in range(4):
                            pb = vcnt % 2
                            vi = vcnt % 3
                            vcnt += 1
                            for k in range(KD):
                                S.op("pe", (lambda e, o=pv[pb], w=wvb[slot], x=xn[s], k=k, tb=tb:
                                            e.matmul(o[:], x[:, k, tb * P:(tb + 1) * P], w[:, k, :],
                                                     start=(k == 0), stop=(k == KD - 1))),
                                     reads=[("wv", slot), ("qn", "xn", s, k)], writes=[("pv", pb)])
                            S.op("act", (lambda e, o=vb[vi], i=pv[pb]: e.copy(o[:], i[:])),
                                 reads=[("pv", pb)], writes=[("vb", vi)])
                            r0 = t0 + s * TS + tb * P
                            S.dma("sp", self.vs[r0:r0 + P, g * 512:(g + 1) * 512], vb[vi][:],
                                  reads=[("vb", vi)], sem=f"qvs{vi}")
            S.emit()

    def ph_att(self, rng):
        nc, S = self.nc, self.S
        scale = 1.0 / math.sqrt(128.0)
        AT = 4 * TS

        def sl(base, n, step):
            return slice(base, base + (n - 1) * step + 1, step)

        with ExitStack() as st:
            qt = [st.enter_context(nc.sbuf_tensor(f"a_qt{i}", [P, AT], BF16)) for i in range(2)]
            kt = [st.enter_context(nc.sbuf_tensor(f"a_kt{i}", [P, 2 * AT], BF16)) for i in range(2)]
            vt = [st.enter_context(nc.sbuf_tensor(f"a_vt{i}", [P, 32, P], BF16)) for i in range(2)]
            oq = st.enter_context(nc.sbuf_tensor("a_oq", [3, AT], BF16))
            ok_ = st.enter_context(nc.sbuf_tensor("a_ok", [3, 2 * AT], BF16))
            nd = st.enter_context(nc.sbuf_tensor("a_nd", [P, 3, 2, AT], F32))
            dt_ = st.enter_context(nc.sbuf_tensor("a_dt", [P, AT], F32))
            ob = [st.enter_context(nc.sbuf_tensor(f"a_ob{i}", [P, AT], BF16)) for i in range(2)]
            pt = [st.enter_context(nc.sbuf_tensor(f"a_pt{i}", [P, 256], BF16)) for i in range(3)]
            negb = st.enter_context(nc.sbuf_tensor("a_negb", [P, 1], F32))
            ps_s = [st.enter_context(nc.psum_tensor(f"a_ps{i}", [P, 256], F32)) for i in range(3)]
            ps_n = [st.enter_context(nc.psum_tensor(f"a_pn{i}", [P, 2, P], F32)) for i in range(3)]
            S.op("pool", (lambda e: e.memset(negb[:], -BIG * scale)), writes=["negb"])
            hcnt = 0
            qbc = 0
            ocnt = 0
            for a0 in att_tiles(rng):
                T0 = a0 * TS
                S.dma("pool", oq[:], self.ohq[:, T0:T0 + AT], writes=["oq"], sem="aoq")
                S.dma("pool", ok_[:], self.ohk[:, T0 - 1024:T0 + AT + 1024], writes=["ok"], sem="aoq")
                for h in range(4):
                    for g in range(3):
                        hd = g * 4 + h
                        d = DIL[g]
                        halo = 64 * d
                        nqb = 16 // d
                        nb = nqb + 1
                        hb = hcnt % 2
                        hcnt += 1
                        S.dma("sp", qt[hb][:], self.qs[hd * P:(hd + 1) * P, T0:T0 + AT], writes=[("qt", hb)], sem=f"aq{hb}")
                        S.dma("sp", kt[hb][:, 0:AT + 2 * halo], self.ks[hd * P:(hd + 1) * P, T0 - halo:T0 + AT + halo],
                              writes=[("kt", hb)], sem=f"ak{hb}")
                        for r in range(d):
                            start = T0 - halo + r
                            src = self.vs[sl(start, P * nb, d), hd * P:(hd + 1) * P].rearrange("(b j) c -> j b c", j=P)
                            S.dma("sp", vt[hb][:, r * nb:(r + 1) * nb, :], src, writes=[("vt", hb, r)], sem=f"av{hb}")
                        for r in range(d):
                            for m in range(nqb):
                                qc = sl(P * m * d + r, P, d)
                                kA = sl(P * m * d + r, P, d)
                                kB = sl(P * (m + 1) * d + r, P, d)
                                off = 1024 - halo
                                oA = sl(P * m * d + r + off, P, d)
                                oB = sl(P * (m + 1) * d + r + off, P, d)
                                pi_ = qbc % 3
                                qbc += 1
                                for half, kc, oc in ((0, kA, oA), (1, kB, oB)):
                                    o_ap = (lambda half=half, pi_=pi_: ps_s[pi_][:, half * P:(half + 1) * P])
                                    S.op("pe", (lambda e, oa=o_ap, kc=kc, qc=qc, hb=hb:
                                                e.matmul(oa(), kt[hb][:, kc], qt[hb][:, qc], start=True, stop=False)),
                                         reads=[("kt", hb), ("qt", hb)], writes=[("pss", pi_)])
                                    S.op("pe", (lambda e, oa=o_ap, oc=oc, qc=qc:
                                                e.matmul(oa(), ok_[0:3, oc], oq[0:3, qc], start=False, stop=False)),
                                         reads=["ok", "oq"], writes=[("pss", pi_)])
                                    S.op("pe", (lambda e, oa=o_ap, half=half:
                                                e.matmul(oa(), self.ident[:], self.maskb[:, half * P:(half + 1) * P],
                                                         start=False, stop=True)),
                                         reads=["ident", "maskb"], writes=[("pss", pi_)])
                                S.op("act", (lambda e, o=pt[pi_], i=ps_s[pi_]:
                                             e.activation(out=o[:], in_=i[:], func=AF.Exp, bias=negb[:, 0:1], scale=scale)),
                                     reads=[("pss", pi_), "negb"], writes=[("pt", pi_)])
                                bA = r * nb + m
                                S.op("pe", (lambda e, o=ps_n[pi_], v=vt[hb], p_=pt[pi_], bA=bA:
                                            e.matmul(o[:, 0, :], v[:, bA, :], p_[:, 0:P], start=True, stop=False)),
                                     reads=[("vt", hb, r), ("pt", pi_)], writes=[("psn", pi_)])
                                S.op("pe", (lambda e, o=ps_n[pi_], v=vt[hb], p_=pt[pi_], bA=bA:
                                            e.matmul(o[:, 0, :], v[:, bA + 1, :], p_[:, P:2 * P], start=False, stop=True)),
                                     reads=[("vt", hb, r), ("pt", pi_)], writes=[("psn", pi_)])
                                S.op("pe", (lambda e, o=ps_n[pi_], p_=pt[pi_]:
                                            e.matmul(o[:, 1, :], self.ones[:], p_[:, 0:P], start=True, stop=False)),
                                     reads=["ones", ("pt", pi_)], writes=[("psn", pi_)])
                                S.op("pe", (lambda e, o=ps_n[pi_], p_=pt[pi_]:
                                            e.matmul(o[:, 1, :], self.ones[:], p_[:, P:2 * P], start=False, stop=True)),
                                     reads=["ones", ("pt", pi_)], writes=[("psn", pi_)])
                                S.op("dve", (lambda e, i=ps_n[pi_], g=g, qc=qc:
                                             e.tensor_copy(nd[:, g, :, qc], i[:])),
                                     reads=[("psn", pi_)], writes=[("nd", g)])
                    S.op("dve", (lambda e: e.tensor_tensor(dt_[:], nd[:, 0, 1, :], nd[:, 1, 1, :], ALU.add)),
                         reads=[("nd", 0), ("nd", 1)], writes=["dt"])
                    S.op("pool", (lambda e: e.tensor_tensor(dt_[:], dt_[:], nd[:, 2, 1, :], ALU.add)),
                         reads=["dt", ("nd", 2)], writes=["dt"])
                    S.op("dve", (lambda e: e.reciprocal(dt_[:], dt_[:])), reads=["dt"], writes=["dt"])
                    for g in range(3):
                        hd = g * 4 + h
                        oi = ocnt % 2
                        ocnt += 1
                        eng = "dve" if g != 1 else "pool"
                        S.op(eng, (lambda e, o=ob[oi], g=g: e.tensor_tensor(o[:], nd[:, g, 0, :], dt_[:], ALU.mult)),
                             reads=[("nd", g), "dt"], writes=[("ob", oi)])
                        S.dma("sp", self.os_[hd * P:(hd + 1) * P, T0:T0 + AT], ob[oi][:], reads=[("ob", oi)], sem=f"ao{oi}")
            S.emit()

    def emit_proj_residual(self, st, tag, in_tiles, nk, w_dram, bias_key, t0, state):
        nc, S = self.nc, self.S
        if "wb" not in state:
            state["wb"] = [st.enter_context(nc.sbuf_tensor(f"{tag}_w{i}", [P, nk, P], BF16)) for i in range(3)]
            state["xr"] = [st.enter_context(nc.sbuf_tensor(f"{tag}_xr{i}", [P, TS], F32)) for i in range(2)]
            state["yo"] = [st.enter_context(nc.sbuf_tensor(f"{tag}_yo{i}", [P, TS], F32)) for i in range(2)]
            state["pd"] = [st.enter_context(nc.psum_tensor(f"{tag}_pd{i}", [P, TS], F32)) for i in range(2)]
            state["wc"] = 0
            state["rc"] = 0
        wb, xr, yo, pd = state["wb"], state["xr"], state["yo"], state["pd"]
        xv = self.xview(self.xs)
        for o_ in range(KD):
            slot = state["wc"] % 3
            state["wc"] += 1
            self.wload(wb, slot, w_dram[o_], (tag, "w"), f"{tag}w")
            for s in range(len(in_tiles)):
                ts0 = t0 + s * TS
                rb = state["rc"] % 2
                state["rc"] += 1
                S.dma("sp", xr[rb][:], xv[:, o_, ts0:ts0 + TS], writes=[(tag, "xr", rb)], sem=f"{tag}xr{rb}")
                for k in range(nk):
                    S.op("pe", (lambda e, o=pd[s], w=wb[slot], x=in_tiles[s], k=k:
                                e.matmul(o[:], w[:, k, :], x[:, k, :], start=(k == 0), stop=(k == nk - 1))),
                         reads=[((tag, "w"), slot), (tag, "in", s, k)], writes=[(tag, "pd", s)])
                if bias_key is None:
                    S.op("dve", (lambda e, o=yo[rb], i=pd[s], x=xr[rb]:
                                 e.tensor_tensor(o[:], i[:], x[:], ALU.add)),
                         reads=[(tag, "pd", s), (tag, "xr", rb)], writes=[(tag, "yo", rb)])
                else:
                    S.op("dve", (lambda e, o=yo[rb], i=pd[s], x=xr[rb], o_=o_:
                                 e.scalar_tensor_tensor(out=o[:], in0=i[:], scalar=self.vcol(bias_key, o_), in1=x[:],
                                                        op0=ALU.add, op1=ALU.add)),
                         reads=[(tag, "pd", s), (tag, "xr", rb), "vec"], writes=[(tag, "yo", rb)])
                S.dma("sp", xv[:, o_, ts0:ts0 + TS], yo[rb][:], reads=[(tag, "yo", rb)], sem=f"{tag}yo{rb}")

    def ph_wo(self, j, rng):
        nc, S = self.nc, self.S
        a, b = rng
        assert (b - a) % 2 == 0
        with ExitStack() as st:
            ot = [st.enter_context(nc.sbuf_tensor(f"o_ot{s}", [P, NHEAD, TS], BF16)) for s in range(2)]
            ov = self.os_.rearrange("(k p) t -> p k t", p=P)
            state = {}
            for pi in range((b - a) // 2):
                t0 = (a + 2 * pi) * TS
                for s in range(2):
                    ts0 = t0 + s * TS
                    S.dma("sp", ot[s][:], ov[:, :, ts0:ts0 + TS],
                          writes=[("wo", "in", s, k) for k in range(NHEAD)], sem=f"oot{s}")
                self.emit_proj_residual(st, "wo", ot, NHEAD, self.wo[j], None, t0, state)
            S.emit()

    def ph_c1(self, l, j, rng):
        nc, S = self.nc, self.S
        a, b = rng
        assert (b - a) % 2 == 0
        with ExitStack() as st:
            xn = [st.enter_context(nc.sbuf_tensor(f"c_xn{s}", [P, KD, TS], BF16)) for s in range(2)]
            NW = 3
            wa = [st.enter_context(nc.sbuf_tensor(f"c_wa{i}", [P, KD, P], BF16)) for i in range(NW)]
            wb_ = [st.enter_context(nc.sbuf_tensor(f"c_wb{i}", [P, KD, P], BF16)) for i in range(NW)]
            sb = [st.enter_context(nc.sbuf_tensor(f"c_sb{i}", [P, TS], F32)) for i in range(2)]
            gl = [st.enter_context(nc.sbuf_tensor(f"c_gl{i}", [P, TS], F32)) for i in range(3)]
            ps_stat = st.enter_context(nc.psum_tensor("c_pstat", [P, TS], F32))
            pa = [st.enter_context(nc.psum_tensor(f"c_pa{i}", [P, TS], F32)) for i in range(2)]
            pb = [st.enter_context(nc.psum_tensor(f"c_pb{i}", [P, TS], F32)) for i in range(2)]
            norm = self.emit_norm(st, self.xs, 0, 2, None, xn, ps_stat, "cn")
            gv = self.xview(self.gs)
            wcnt = 0
            gcnt = 0
            for pi in range((b - a) // 2):
                t0 = (a + 2 * pi) * TS
                norm(t0, ("mix_norm", l))
                for cc in range(KD):
                    slot = wcnt % NW
                    wcnt += 1
                    self.wload(wa, slot, self.win[j, cc], "wa", "cwa")
                    self.wload(wb_, slot, self.win[j, KD + cc], "wb", "cwb")
                    for s in range(2):
                        ts0 = t0 + s * TS
                        for k in range(KD):
                            S.op("pe", (lambda e, o=pa[s], w=wa[slot], x=xn[s], k=k:
                                        e.matmul(o[:], w[:, k, :], x[:, k, :], start=(k == 0), stop=(k == KD - 1))),
                                 reads=[("wa", slot), ("cn", "xn", s, k)], writes=[("pa", s)])
                        for k in range(KD):
                            S.op("pe", (lambda e, o=pb[s], w=wb_[slot], x=xn[s], k=k:
                                        e.matmul(o[:], w[:, k, :], x[:, k, :], start=(k == 0), stop=(k == KD - 1))),
                                 reads=[("wb", slot), ("cn", "xn", s, k)], writes=[("pb", s)])
                        gi = gcnt % 3
                        gcnt += 1
                        S.op("act", (lambda e, o=sb[s], i=pb[s], cc=cc:
                                     e.activation(out=o[:], in_=i[:], func=AF.Sigmoid,
                                                  bias=self.vcol(("b_in", j), KD + cc), scale=1.0)),
                             reads=[("pb", s), "vec"], writes=[("sb", s)])
                        S.op("dve", (lambda e, o=gl[gi], i=pa[s], g_=sb[s], cc=cc:
                                     e.scalar_tensor_tensor(out=o[:], in0=i[:], scalar=self.vcol(("b_in", j), cc),
                                                            in1=g_[:], op0=ALU.add, op1=ALU.mult)),
                             reads=[("pa", s), ("sb", s), "vec"], writes=[("gl", gi)])
                        S.dma("sp", gv[:, cc, ts0:ts0 + TS], gl[gi][:], reads=[("gl", gi)], sem=f"cgl{gi}")
            S.emit()

    def ph_c2(self, j, rng):
        nc, S = self.nc, self.S
        a, b = rng
        assert (b - a) % 2 == 0
        GW = TS + CW - 1
        with ExitStack() as st:
            hcv = [st.enter_context(nc.sbuf_tensor(f"d_hcv{s}", [P, KD, TS], F32)) for s in range(2)]
            hn = [st.enter_context(nc.sbuf_tensor(f"d_hn{s}", [P, KD, TS], BF16)) for s in range(2)]
            gt = [st.enter_context(nc.sbuf_tensor(f"d_gt{i}", [P, GW], F32)) for i in range(4)]
            acc = {e: [st.enter_context(nc.sbuf_tensor(f"d_acc{e}{i}", [P, TS], F32)) for i in range(2)]
                   for e in ("dve", "pool")}
            hb = [st.enter_context(nc.sbuf_tensor(f"d_hb{i}", [P, TS], BF16)) for i in range(2)]
            sq = [st.enter_context(nc.sbuf_tensor(f"d_sq{i}", [P, TS], BF16)) for i in range(2)]
            mean = st.enter_context(nc.sbuf_tensor("d_mean", [P, TS], F32))
            msq = st.enter_context(nc.sbuf_tensor("d_msq", [P, TS], F32))
            rstd = st.enter_context(nc.sbuf_tensor("d_rstd", [P, TS], F32))
            ps_sum = st.enter_context(nc.psum_tensor("d_psum", [P, TS], F32))
            ps_sq = st.enter_context(nc.psum_tensor("d_psq", [P, TS], F32))
            gv = self.xview(self.gs)
            state = {}
            gcnt = 0
            wbase = self.VEC[("w_dw", j)]
            for pi in range((b - a) // 2):
                t0 = (a + 2 * pi) * TS
                for s in range(2):
                    t = a + 2 * pi + s
                    ts0 = t * TS
                    for cc in range(KD):
                        gi = gcnt % 4
                        gcnt += 1
                        S.dma("sp", gt[gi][:], gv[:, cc, ts0 - 15:ts0 - 15 + GW], writes=[("gt", gi)], sem=f"dgt{gi}")
                        S.op("pool", (lambda e, g_=gt[gi], t=t:
                                      e.tensor_scalar(g_[:, 0:15], g_[:, 0:15], self.flag_sb[:, 2 * t:2 * t + 1], None, ALU.mult)),
                             reads=[("gt", gi), "flag"], writes=[("gt", gi)])
                        S.op("pool", (lambda e, g_=gt[gi], t=t:
                                      e.tensor_scalar(g_[:, TS + 15:GW], g_[:, TS + 15:GW],
                                                      self.flag_sb[:, 2 * t + 1:2 * t + 2], None, ALU.mult)),
                             reads=[("gt", gi), "flag"], writes=[("gt", gi)])
                        en = "dve"
                        ac = acc[en]

                        def wcol(k, cc=cc):
                            c0 = wbase + cc * CW + k
                            return self.vec_sb[:, c0:c0 + 1]
                        S.op(en, (lambda e, o=ac[0], g_=gt[gi], cc=cc, wcol=wcol:
                                  e.tensor_scalar(o[:], g_[:, 0:TS], wcol(0), self.vcol(("b_dw", j), cc), ALU.mult, ALU.add)),
                             reads=[("gt", gi), "vec"], writes=[("acc", en, 0)])
                        S.op(en, (lambda e, o=ac[1], g_=gt[gi], wcol=wcol:
                                  e.tensor_scalar(o[:], g_[:, 1:1 + TS], wcol(1), None, ALU.mult)),
                             reads=[("gt", gi), "vec"], writes=[("acc", en, 1)])
                        for k in range(2, CW):
                            S.op(en, (lambda e, o=ac[k % 2], g_=gt[gi], k=k, wcol=wcol:
                                      e.scalar_tensor_tensor(out=o[:], in0=g_[:, k:k + TS], scalar=wcol(k), in1=o[:],
                                                             op0=ALU.mult, op1=ALU.add)),
                                 reads=[("gt", gi), ("acc", en, k % 2), "vec"], writes=[("acc", en, k % 2)])
                        S.op(en, (lambda e, o=hcv[s], cc=cc, ac=ac: e.tensor_tensor(o[:, cc, :], ac[0][:], ac[1][:], ALU.add)),
                             reads=[("acc", en, 0), ("acc", en, 1)], writes=[("hcv", s, cc)])
                        q = cc % 2
                        S.op("act", (lambda e, o=hb[q], i=hcv[s], cc=cc: e.copy(o[:], i[:, cc, :])),
                             reads=[("hcv", s, cc)], writes=[("hb", q)])
                        S.op("pe", (lambda e, i=hb[q], cc=cc:
                                    e.matmul(ps_sum[:], self.ones[:], i[:], start=(cc == 0), stop=(cc == KD - 1))),
                             reads=[("hb", q), "ones"], writes=["psum"])
                        S.op("act", (lambda e, o=sq[q], i=hcv[s], cc=cc:
                                     e.activation(out=o[:], in_=i[:, cc, :], func=AF.Square)),
                             reads=[("hcv", s, cc)], writes=[("sq", q)])
                        S.op("pe", (lambda e, i=sq[q], cc=cc:
                                    e.matmul(ps_sq[:], self.ones[:], i[:], start=(cc == 0), stop=(cc == KD - 1))),
                             reads=[("sq", q), "ones"], writes=["psq"])
                    S.op("dve", (lambda e: e.tensor_scalar(mean[:], ps_sum[:], 1.0 / D, None, ALU.mult)),
                         reads=["psum"], writes=["mean"])
                    S.op("dve", (lambda e: e.tensor_tensor(msq[:], mean[:], mean[:], ALU.mult)),
                         reads=["mean"], writes=["msq"])
                    S.op("dve", (lambda e: e.scalar_tensor_tensor(out=msq[:], in0=ps_sq[:], scalar=1.0 / D, in1=msq[:],
                                                                  op0=ALU.mult, op1=ALU.subtract)),
                         reads=["psq", "msq"], writes=["msq"])
                    S.op("act", (lambda e: e.activation(out=rstd[:], in_=msq[:], func=AF.Sqrt,
                                                        bias=self.epsc[:, 1:2], scale=1.0)),
                         reads=["msq", "epsc"], writes=["rstd"])
                    S.op("dve", (lambda e: e.reciprocal(rstd[:], rstd[:])), reads=["rstd"], writes=["rstd"])
                    for cc in range(KD):
                        S.op("dve", (lambda e, o=hcv[s], cc=cc: e.tensor_tensor(o[:, cc, :], o[:, cc, :], mean[:], ALU.subtract)),
                             reads=[("hcv", s, cc), "mean"], writes=[("hcv", s, cc)])
                        S.op("pool", (lambda e, o=hcv[s], cc=cc: e.tensor_tensor(o[:, cc, :], o[:, cc, :], rstd[:], ALU.mult)),
                             reads=[("hcv", s, cc), "rstd"], writes=[("hcv", s, cc)])
                        S.op("act", (lambda e, o=hn[s], i=hcv[s], cc=cc:
                                     e.activation(out=o[:, cc, :], in_=i[:, cc, :], func=AF.Silu,
                                                  bias=self.vcol(("ln_b", j), cc), scale=self.vcol(("ln_g", j), cc))),
                             reads=[("hcv", s, cc), "vec"], writes=[("c2", "in", s, cc)])
                self.emit_proj_residual(st, "c2", hn, KD, self.wout[j], ("b_out", j), t0, state)
            S.emit()

    def ph_copy(self, rng):
        nc, S = self.nc, self.S
        a, b = rng
        with ExitStack() as st:
            buf = [st.enter_context(nc.sbuf_tensor(f"cp{i}", [P, KD, TS], F32)) for i in range(2)]
            sv = self.xview(self.xin)
            dv = self.xview(self.xs)
            for t in range(a, b):
                i = t % 2
                S.dma("sp", buf[i][:], sv[:, :, t * TS:(t + 1) * TS], writes=[("cp", i)], sem=f"cpl{i}")
                S.dma("sp", dv[:, :, t * TS:(t + 1) * TS], buf[i][:], reads=[("cp", i)], sem=f"cps{i}")
            S.emit()

    def ph_final(self, rng):
        nc, S = self.nc, self.S
        a, b = rng
        o0 = self.cfg.own[0]
        with ExitStack() as st:
            ps_stat = st.enter_context(nc.psum_tensor("z_pstat", [P, TS], F32))
            rstd = st.enter_context(nc.sbuf_tensor("z_rstd", [P, TS], F32))
            xg = [st.enter_context(nc.sbuf_tensor(f"z_xg{i}", [P, 2, TS], F32)) for i in range(4)]
            sq = [st.enter_context(nc.sbuf_tensor(f"z_sq{i}", [P, TS], BF16)) for i in range(2)]
            yo = [st.enter_context(nc.sbuf_tensor(f"z_yo{i}", [P, 2, TS], F32)) for i in range(3)]
            srcv = self.xview(self.xs)
            outv = self.xview(self.yout)
            cnt = 0
            ycnt = 0
            for t in range(a, b):
                ts0 = t * TS
                for kg in range(KD // 2):
                    bb = cnt % 4
                    cnt += 1
                    S.dma("sp", xg[bb][:], srcv[:, kg * 2:kg * 2 + 2, ts0:ts0 + TS], writes=[("xg", bb)], sem=f"zxg{bb}")
                    for kk in range(2):
                        k = kg * 2 + kk
                        q = k % 2
                        S.op("act", (lambda e, o=sq[q], i=xg[bb], kk=kk:
                                     e.activation(out=o[:], in_=i[:, kk, :], func=AF.Square)),
                             reads=[("xg", bb)], writes=[("sq", q)])
                        S.op("pe", (lambda e, i=sq[q], k=k:
                                    e.matmul(ps_stat[:], self.ones[:], i[:], start=(k == 0), stop=(k == KD - 1))),
                             reads=[("sq", q), "ones"], writes=["pstat"])
                S.op("act", (lambda e: e.activation(out=rstd[:], in_=ps_stat[:], func=AF.Sqrt,
                                                    bias=self.epsc[:, 0:1], scale=1.0 / D)),
                     reads=["pstat", "epsc"], writes=["rstd"])
                S.op("dve", (lambda e: e.reciprocal(rstd[:], rstd[:])),
                     reads=["rstd"], writes=["rstd"])
                for kg in range(KD // 2):
                    bb = cnt % 4
                    cnt += 1
                    S.dma("sp", xg[bb][:], srcv[:, kg * 2:kg * 2 + 2, ts0:ts0 + TS], writes=[("xg", bb)], sem=f"zxg{bb}")
                    yb = ycnt % 3
                    ycnt += 1
                    for kk in range(2):
                        k = kg * 2 + kk
                        S.op("dve", (lambda e, o=yo[yb], i=xg[bb], kk=kk, k=k:
                                     e.scalar_tensor_tensor(out=o[:, kk, :], in0=i[:, kk, :],
                                                            scalar=self.vcol(("final_norm",), k), in1=rstd[:],
                                                            op0=ALU.mult, op1=ALU.mult)),
                             reads=[("xg", bb), "rstd", "vec"], writes=[("yo", yb)])
                    oc = (t - o0) * TS
                    S.dma("sp", outv[:, kg * 2:kg * 2 + 2, oc:oc + TS], yo[yb][:], reads=[("yo", yb)], sem=f"zyo{yb}")
            S.emit()


def _bf16_exact(a):
    return a.astype(np.float32)


def host_consts():
    ident = np.eye(P, dtype=np.float32)
    ones = np.ones((P, P), np.float32)
    perm = np.zeros((P, P), np.float32)
    for i in range(16):
        perm[i + 16, i] = 1.0
        perm[i, i + 16] = 1.0
    mb = np.zeros((P, 256), np.float32)
    j = np.arange(P)[:, None]
    i = np.arange(P)[None, :]
    mb[:, 0:128] = np.where(j >= i, 0.0, -BIG)
    mb[:, 128:256] = np.where(j <= i, 0.0, -BIG)
    return np.concatenate([ident, ones, perm, mb], axis=1)


def pk(v):
    v = np.asarray(v, np.float32)
    return np.ascontiguousarray(v.reshape(-1, P).T)


def host_vecs(inp):
    n = Builder.n_vec_cols()
    V = Builder.VEC
    t = np.zeros((P, n), np.float32)

    def put(key, arr):
        t[:, V[key]:V[key] + arr.shape[1]] = arr
    for l in range(DEPTH):
        for f in range(2):
            put(("ffn_norm", l, f), pk(inp["ffn_norm"][l, f]))
        put(("mix_norm", l), pk(inp["mix_norm"][l]))
    for j in range(2):
        put(("b_in", j), pk(inp["conv_b_in"][j]))
        wdw = np.asarray(inp["conv_w_dw"][j], np.float32)
        put(("w_dw", j), np.ascontiguousarray(wdw.reshape(CW, KD, P).transpose(2, 1, 0)).reshape(P, KD * CW))
        put(("b_dw", j), pk(inp["conv_b_dw"][j]))
        put(("ln_g", j), pk(inp["conv_ln_g"][j]))
        put(("ln_b", j), pk(inp["conv_ln_b"][j]))
        put(("b_out", j), pk(inp["conv_b_out"][j]))
    put(("final_norm",), pk(inp["final_norm"]))
    return t


def relayout_w(w, nout_chunk=P):
    w = np.asarray(w, np.float32)
    K, N = w.shape
    a = w.reshape(K // P, P, N // nout_chunk, nout_chunk).transpose(2, 1, 0, 3)
    return np.ascontiguousarray(a).reshape(N // nout_chunk, P, (K // P) * nout_chunk)


def host_weights(inp):
    out = {}
    out["wg"] = np.stack([relayout_w(inp["ffn_w_gate"][l, f]) for l in range(DEPTH) for f in range(2)])
    out["wu"] = np.stack([relayout_w(inp["ffn_w_up"][l, f]) for l in range(DEPTH) for f in range(2)])
    out["wd"] = np.stack([relayout_w(inp["ffn_w_down"][l, f]) for l in range(DEPTH) for f in range(2)])
    wqkv = np.asarray(inp["attn_w_qkv"], np.float32)
    out["wqk"] = np.stack([relayout_w(wqkv[j][:, :2 * AW]) for j in range(2)])
    out["wv"] = np.stack([relayout_w(wqkv[j][:, 2 * AW:], 512) for j in range(2)])
    out["wo"] = np.stack([relayout_w(inp["attn_w_o"][j]) for j in range(2)])
    out["win"] = np.stack([relayout_w(inp["conv_w_in"][j]) for j in range(2)])
    out["wout"] = np.stack([relayout_w(inp["conv_w_out"][j]) for j in range(2)])
    return out


def core_tables(c, nt=NT, halo=HALO):
    ll = nt * TS
    g = OWN * c - halo + np.arange(ll)
    valid = (g >= 0) & (g < NSEQ * SEQ)
    seg = np.where(valid, g // SEQ, -1)
    pos = np.where(valid, g % SEQ, 0).astype(np.float32)
    freqs = (500000.0 ** (-(np.arange(0, 32, 2, dtype=np.float32) / np.float32(32.0)))).astype(np.float32)
    ang = (pos[None, :] * freqs[:, None]).astype(np.float32)
    cos = np.cos(ang).astype(np.float32)
    sin = np.sin(ang).astype(np.float32)
    cosT = np.concatenate([cos, cos], axis=0)
    sinT = np.concatenate([-sin, sin], axis=0)
    segs = [s for s in np.unique(seg) if s >= 0]
    cls = np.full(ll, 2)
    for i, s in enumerate(segs[:2]):
        cls[seg == s] = i
    assert len(segs) <= 2
    oh = np.zeros((3, ll), np.float32)
    oh[cls, np.arange(ll)] = 1.0
    tcls = cls.reshape(nt, TS)[:, 0]
    fl = np.zeros((nt, 2), np.float32)
    for t in range(nt):
        fl[t, 0] = 1.0 if (t > 0 and tcls[t - 1] == tcls[t]) else 0.0
        fl[t, 1] = 1.0 if (t < nt - 1 and tcls[t + 1] == tcls[t]) else 0.0
    flags = np.broadcast_to(fl.reshape(1, nt * 2), (P, nt * 2)).copy()
    return dict(cosT=cosT, sinT=sinT, ohk=oh, ohq=oh * BIG, flags=flags, g=g, valid=valid)


def core_x(xflat, c, nt=NT, halo=HALO):
    ll = nt * TS
    g0 = OWN * c - halo
    lo = max(g0, 0)
    hi = min(g0 + ll, xflat.shape[0])
    out = np.zeros((D, ll), np.float32)
    out[:, lo - g0:hi - g0] = xflat[lo:hi].T
    return out


_CACHE = {}


def kernel(**inputs):
    inp = {k: np.asarray(v) for k, v in inputs.items()}
    xflat = np.concatenate([inp["x_prompt"].reshape(-1, D), inp["x_sample"].reshape(-1, D)], axis=0)
    cfg = Cfg()
    nc = Builder(cfg).build()
    W = host_weights(inp)
    vecs = host_vecs(inp)
    consts = host_consts()
    in_maps = []
    for c in range(NCORES):
        tb = core_tables(c)
        m = dict(W)
        m.update(xin=core_x(xflat, c), vecs=vecs, consts=consts, cosT=tb["cosT"], sinT=tb["sinT"],
                 ohk=tb["ohk"], ohq=tb["ohq"], flags=tb["flags"])
        in_maps.append(m)
    res = run_bass_kernel_spmd(nc, in_maps, core_ids=list(range(NCORES)))
    y = np.concatenate([np.asarray(r["yT"]).T for r in res.results], axis=0)
    y = np.ascontiguousarray(y, dtype=np.float32)
    yp = y[:2 * SEQ].reshape(2, SEQ, D)
    ysm = y[2 * SEQ:].reshape(1, SEQ, D)
    return (yp, ysm)
```

```python
import math
from contextlib import ExitStack

import numpy as np
import concourse.bass as bass
import concourse.mybir as mybir
from concourse.bass_utils import run_bass_kernel_spmd

F32 = mybir.dt.float32
BF16 = mybir.dt.bfloat16
AF = mybir.ActivationFunctionType
ALU = mybir.AluOpType

P = 128
D = 2048
KD = 16
DFF = 5632
KF = 44
TS = 512
NHEAD = 12
AW = 1536
CW = 31
DEPTH = 4
SEQ = 16384
NSEQ = 3
NCORES = 8
OWN = 6144
HALO = 2560
NT = (OWN + 2 * HALO) // TS
LL = NT * TS
OWN_T0 = HALO // TS
OWN_T1 = OWN_T0 + OWN // TS
BIG = 2048.0
RMS_EPS = 1e-6
LN_EPS = 1e-5
DIL = (1, 4, 16)
ENGS = ("pe", "act", "dve", "pool", "sp")


class Op:
    __slots__ = ("eng", "fn", "waits", "is_dma", "sem", "semval", "need_sig", "sig", "tag")

    def __init__(self, eng, fn, is_dma=False):
        self.eng = eng
        self.fn = fn
        self.waits = []
        self.is_dma = is_dma
        self.sem = None
        self.semval = 0
        self.need_sig = False
        self.sig = 0
        self.tag = None


class Sched:
    def __init__(self, nc, stack):
        self.nc = nc
        self.esem = {e: stack.enter_context(nc.semaphore("sg_" + e)) for e in ENGS}
        self.ecount = {e: 0 for e in ENGS}
        self.dsem = {}
        self.dcount = {}
        self.stack = stack
        self.begin()

    def dma_sem(self, name):
        if name not in self.dsem:
            self.dsem[name] = self.stack.enter_context(self.nc.semaphore("dm_" + name))
            self.dcount[name] = 0
        return name

    def begin(self):
        self.q = {e: [] for e in ENGS}
        self.res = {}

    def _deps(self, o, reads, writes):
        res = self.res
        deps = []
        for k in reads:
            st = res.get(k)
            if st is not None and st[0] is not None:
                deps.append((st[0], "raw"))
        for k in writes:
            st = res.get(k)
            if st is not None:
                if st[0] is not None:
                    deps.append((st[0], "waw"))
                for r in st[1]:
                    deps.append((r, "war"))
        for k in reads:
            st = res.get(k)
            if st is None:
                res[k] = [None, [o]]
            else:
                st[1].append(o)
        for k in writes:
            res[k] = [o, []]
        for d, kind in deps:
            if d is o:
                continue
            if not d.is_dma and d.eng == o.eng and not o.is_dma:
                if o.eng == "pe":
                    continue
                if kind != "raw":
                    continue
            o.waits.append(d)
            if not d.is_dma:
                d.need_sig = True

    def op(self, eng, fn, reads=(), writes=()):
        o = Op(eng, fn)
        o.tag = (tuple(writes), tuple(reads))
        self._deps(o, reads, writes)
        self.q[eng].append(o)
        return o

    def dma(self, eng, out, in_, reads=(), writes=(), sem=None):
        name = self.dma_sem(sem)
        o = Op(eng, (lambda e, out=out, in_=in_: e.dma_start(out=out, in_=in_)), is_dma=True)
        o.tag = (tuple(writes), tuple(reads), sem)
        self._deps(o, reads, writes)
        self.dcount[name] += 16
        o.sem = name
        o.semval = self.dcount[name]
        self.q[eng].append(o)
        return o

    def check(self):
        ptr = {e: 0 for e in ENGS}
        done = set()
        total = sum(len(v) for v in self.q.values())
        ndone = 0
        while ndone < total:
            prog = False
            for e in ENGS:
                q = self.q[e]
                while ptr[e] < len(q):
                    o = q[ptr[e]]
                    if all(id(d) in done for d in o.waits):
                        done.add(id(o))
                        ptr[e] += 1
                        ndone += 1
                        prog = True
                    else:
                        break
            if not prog:
                msg = []
                for e in ENGS:
                    if ptr[e] < len(self.q[e]):
                        o = self.q[e][ptr[e]]
                        blk = [(d.eng, d.is_dma, getattr(d, "tag", None)) for d in o.waits if id(d) not in done]
                        msg.append(f"{e}@{ptr[e]} tag={getattr(o, 'tag', None)} blocked on {blk}")
                raise RuntimeError("DEADLOCK in schedule: " + " | ".join(msg))

    def emit(self, final=False):
        nc = self.nc
        self.check()
        for e in ENGS:
            lastc = None
            for o in self.q[e]:
                if not o.is_dma:
                    lastc = o
            if lastc is not None:
                lastc.need_sig = True
            c = self.ecount[e]
            for o in self.q[e]:
                if not o.is_dma and o.need_sig:
                    c += 1
                    o.sig = c
            self.ecount[e] = c
        end_e = dict(self.ecount)
        end_d = dict(self.dcount)
        esem, dsem = self.esem, self.dsem

        def replay(e, eng):
            waited_e = {}
            waited_d = {}
            for o in self.q[e]:
                for d in o.waits:
                    if d.is_dma:
                        if waited_d.get(d.sem, 0) < d.semval:
                            eng.wait_ge(dsem[d.sem], d.semval)
                            waited_d[d.sem] = d.semval
                    else:
                        if waited_e.get(d.eng, 0) < d.sig:
                            eng.wait_ge(esem[d.eng], d.sig)
                            waited_e[d.eng] = d.sig
                ins = o.fn(eng)
                if o.is_dma:
                    ins.then_inc(dsem[o.sem], 16)
                elif o.need_sig:
                    ins.then_inc(esem[e], 1)
            for f in ENGS:
                if f != e and end_e[f] > 0 and waited_e.get(f, 0) < end_e[f]:
                    eng.wait_ge(esem[f], end_e[f])
            for name, v in end_d.items():
                if v > 0 and waited_d.get(name, 0) < v:
                    eng.wait_ge(dsem[name], v)

        with nc.Block() as block:
            @block.tensor
            def _(eng):
                replay("pe", eng)

            @block.scalar
            def _(eng):
                replay("act", eng)

            @block.vector
            def _(eng):
                replay("dve", eng)

            @block.gpsimd
            def _(eng):
                replay("pool", eng)

            @block.sync
            def _(eng):
                replay("sp", eng)
        self.begin()


class Cfg:
    def __init__(self, **kw):
        self.nt = NT
        self.own = (OWN_T0, OWN_T1)
        self.depth = DEPTH
        self.phases = None
        self.dbg = ()
        for k, v in kw.items():
            setattr(self, k, v)


def phase_plan():
    pl = []
    r0 = (0, 22)
    r1 = (2, 20)
    r2 = (4, 18)
    r3 = (5, 17)
    o0, o1 = OWN_T0 * TS, OWN_T1 * TS

    def tk(h):
        return (o0 - h, o1 + h)
    pl += [("copy", 0, 0, (0, o0 - 2112)), ("copy", 0, 0, (o1 + 2112, NT * TS)), ("ffn", 0, 0, tk(2112), "xin"), ("qkv", 0, 0, r0), ("att", 0, 0, r1), ("wo", 0, 0, r1), ("ffn", 0, 1, tk(1088))]
    pl += [("ffn", 1, 0, tk(1088)), ("c1", 1, 0, (1, 21)), ("c2", 1, 0, r1), ("ffn", 1, 1, tk(1088))]
    pl += [("ffn", 2, 0, tk(1088)), ("qkv", 2, 1, r1), ("att", 2, 1, r2), ("wo", 2, 1, r2), ("ffn", 2, 1, tk(64))]
    pl += [("ffn", 3, 0, tk(64)), ("c1", 3, 1, r2), ("c2", 3, 1, r3), ("ffn", 3, 1, tk(0))]
    pl += [("final", 0, 0, r3)]
    return pl


def att_tiles(rng):
    a, b = rng
    offs = list(range(a, b - 3, 4))
    if not offs or offs[-1] + 4 < b:
        offs.append(b - 4)
    return offs


class NcProxy:
    def __init__(self, nc):
        self._nc = nc
        self.uid = 0

    def __getattr__(self, name):
        return getattr(self._nc, name)

    def sbuf_tensor(self, name, shape, dt):
        return self._nc.sbuf_tensor(f"{name}_u{self.uid}", shape, dt)

    def psum_tensor(self, name, shape, dt):
        return self._nc.psum_tensor(f"{name}_u{self.uid}", shape, dt)


class Builder:
    def __init__(self, cfg):
        self.cfg = cfg
        self.nc_real = bass.Bass("TRN2", target_bir_lowering=False)
        self.nc = NcProxy(self.nc_real)
        self.top = ExitStack()

    def declare(self):
        nc = self.nc
        c = self.cfg
        ll = c.nt * TS
        self.ll = ll

        def inp(name, shape, dt=F32):
            return nc.dram_tensor(name, list(shape), dt, kind="ExternalInput").ap()

        def scr(name, shape, dt):
            kind = "ExternalOutput" if name in c.dbg else "Internal"
            return nc.dram_tensor(name, list(shape), dt, kind=kind).ap()

        self.xin = inp("xin", (D, ll))
        self.wg = inp("wg", (DEPTH * 2, KF, P, KD * P))
        self.wu = inp("wu", (DEPTH * 2, KF, P, KD * P))
        self.wd = inp("wd", (DEPTH * 2, KD, P, KF * P))
        self.wqk = inp("wqk", (2, 24, P, KD * P))
        self.wv = inp("wv", (2, 3, P, KD * 512))
        self.wo = inp("wo", (2, KD, P, NHEAD * P))
        self.win = inp("win", (2, 32, P, KD * P))
        self.wout = inp("wout", (2, KD, P, KD * P))
        self.vecs = inp("vecs", (P, self.n_vec_cols()))
        self.cosT = inp("cosT", (32, ll))
        self.sinT = inp("sinT", (32, ll))
        self.ohk = inp("ohk", (3, ll))
        self.ohq = inp("ohq", (3, ll))
        self.flags = inp("flags", (P, c.nt * 2))
        self.consts = inp("consts", (P, 3 * P + 256))
        self.xs = scr("xs", (D, ll), F32)
        self.qs = scr("qs", (AW, ll), BF16)
        self.ks = scr("ks", (AW, ll), BF16)
        self.vs = scr("vs", (ll, AW), BF16)
        self.os_ = scr("os", (AW, ll), BF16)
        self.gs = scr("gs", (D, ll), BF16)
        o0, o1 = c.own
        self.yout = nc.dram_tensor("yT", [D, (o1 - o0) * TS], F32, kind="ExternalOutput").ap()

    VEC = {}

    @classmethod
    def n_vec_cols(cls):
        if not cls.VEC:
            col = 0

            def add(name, n):
                nonlocal col
                cls.VEC[name] = col
                col += n
            for l in range(DEPTH):
                for f in range(2):
                    add(("ffn_norm", l, f), KD)
                add(("mix_norm", l), KD)
            for j in range(2):
                add(("b_in", j), 32)
                add(("w_dw", j), KD * CW)
                add(("b_dw", j), KD)
                add(("ln_g", j), KD)
                add(("ln_b", j), KD)
                add(("b_out", j), KD)
            add(("final_norm",), KD)
            cls.VEC["_n"] = col
        return cls.VEC["_n"]

    def xview(self, t):
        return t.rearrange("(k p) t -> p k t", p=P)

    def build(self):
        nc = self.nc
        c = self.cfg
        self.declare()
        top = self.top
        with top:
            S = self.S = Sched(nc, top)
            self.vec_sb = top.enter_context(nc.sbuf_tensor("vec_sb", [P, self.n_vec_cols()], F32))
            self.ident = top.enter_context(nc.sbuf_tensor("ident", [P, P], BF16))
            self.ones = top.enter_context(nc.sbuf_tensor("ones", [P, P], BF16))
            self.perm = top.enter_context(nc.sbuf_tensor("perm", [P, P], BF16))
            self.maskb = top.enter_context(nc.sbuf_tensor("maskb", [P, 256], BF16))
            self.flag_sb = top.enter_context(nc.sbuf_tensor("flag_sb", [P, c.nt * 2], F32))
            self.epsc = top.enter_context(nc.sbuf_tensor("epsc", [P, 2], F32))
            S.op("pool", (lambda e: e.memset(self.epsc[:, 0:1], RMS_EPS)), writes=["epsc"])
            S.op("pool", (lambda e: e.memset(self.epsc[:, 1:2], LN_EPS)), writes=["epsc"])
            S.dma("sp", self.vec_sb[:], self.vecs[:, :], writes=["vec"], sem="c0")
            S.dma("sp", self.flag_sb[:], self.flags[:, :], writes=["flag"], sem="c0")
            S.dma("pool", self.ident[:], self.consts[:, 0:P], writes=["ident"], sem="c1")
            S.dma("pool", self.ones[:], self.consts[:, P:2 * P], writes=["ones"], sem="c1")
            S.dma("pool", self.perm[:], self.consts[:, 2 * P:3 * P], writes=["perm"], sem="c1")
            S.dma("pool", self.maskb[:], self.consts[:, 3 * P:3 * P + 256], writes=["maskb"], sem="c1")
            S.emit()
            plan = c.phases if c.phases is not None else phase_plan()
            for ph in plan:
                kind = ph[0]
                self.nc.uid += 1
                if kind == "ffn":
                    src = self.xin if (len(ph) > 4 and ph[4] == "xin") else self.xs
                    self.ph_ffn(ph[1], ph[2], ph[3], src)
                elif kind == "qkv":
                    self.ph_qkv(ph[1], ph[2], ph[3])
                elif kind == "att":
                    self.ph_att(ph[3])
                elif kind == "wo":
                    self.ph_wo(ph[2], ph[3])
                elif kind == "c1":
                    self.ph_c1(ph[1], ph[2], ph[3])
                elif kind == "c2":
                    self.ph_c2(ph[2], ph[3])
                elif kind == "final":
                    self.ph_final(ph[3])
                elif kind == "copy":
                    self.ph_copy(ph[3])
                else:
                    raise ValueError(kind)
        return self.nc_real

    def vcol(self, key, k=0, n=1):
        c0 = self.VEC[key] + k
        return self.vec_sb[:, c0:c0 + n]

    def emit_norm(self, st, src, t0, ns, gkey, xn, ps_stat, tag):
        nc, S = self.nc, self.S
        srcv = self.xview(src)
        XG = 2
        NXG = 2
        xg = [st.enter_context(nc.sbuf_tensor(f"{tag}_xg{i}", [P, XG, TS], F32)) for i in range(NXG)]
        sq = [st.enter_context(nc.sbuf_tensor(f"{tag}_sq{i}", [P, TS], BF16)) for i in range(2)]
        rstd = [st.enter_context(nc.sbuf_tensor(f"{tag}_rstd{i}", [P, TS], F32)) for i in range(ns)]
        self._norm_bufs = (xg, sq, rstd)

        def stats(t0, src=src, widths=None):
            srcv = self.xview(src)
            if widths is None:
                widths = [TS] * ns
            offs = [sum(widths[:i]) for i in range(len(widths))]
            for s in range(len(widths)):
                ts0 = t0 + offs[s]
                w = widths[s]
                for kg in range(KD // XG):
                    b = cnt[0] % NXG
                    cnt[0] += 1
                    S.dma("sp", xg[b][:, :, :w], srcv[:, kg * XG:(kg + 1) * XG, ts0:ts0 + w],
                          writes=[(tag, "xg", b)], sem=f"{tag}xg{b}")
                    for kk in range(XG):
                        k = kg * XG + kk
                        q = k % 2
                        S.op("act", (lambda e, o=sq[q], i=xg[b], kk=kk, w=w:
                                     e.activation(out=o[:, :w], in_=i[:, kk, :w], func=AF.Square)),
                             reads=[(tag, "xg", b)], writes=[(tag, "sq", q)])
                        S.op("pe", (lambda e, o=ps_stat, i=sq[q], k=k, w=w:
                                    e.matmul(o[:, :w], self.ones[:], i[:, :w], start=(k == 0), stop=(k == KD - 1))),
                             reads=[(tag, "sq", q), "ones"], writes=[(tag, "pstat")])
                S.op("act", (lambda e, o=rstd[s], i=ps_stat, w=w:
                             e.activation(out=o[:, :w], in_=i[:, :w], func=AF.Sqrt, bias=self.epsc[:, 0:1], scale=1.0 / D)),
                     reads=[(tag, "pstat"), "epsc"], writes=[(tag, "rstd", s)])
                S.op("dve", (lambda e, o=rstd[s], w=w: e.reciprocal(o[:, :w], o[:, :w])),
                     reads=[(tag, "rstd", s)], writes=[(tag, "rstd", s)])

        def apply(t0, gkey, src=src, widths=None, xn=xn, xkey="xn"):
            srcv = self.xview(src)
            if widths is None:
                widths = [TS] * ns
            offs = [sum(widths[:i]) for i in range(len(widths))]
            for s in range(len(widths)):
                ts0 = t0 + offs[s]
                w = widths[s]
                for kg in range(KD // XG):
                    b = cnt[0] % NXG
                    cnt[0] += 1
                    S.dma("sp", xg[b][:, :, :w], srcv[:, kg * XG:(kg + 1) * XG, ts0:ts0 + w],
                          writes=[(tag, "xg", b)], sem=f"{tag}xg{b}")
                    for kk in range(XG):
                        k = kg * XG + kk
                        S.op("dve", (lambda e, o=xn[s], i=xg[b], kk=kk, k=k, r=rstd[s], w=w:
                                     e.scalar_tensor_tensor(out=o[:, k, :w], in0=i[:, kk, :w],
                                                            scalar=self.vcol(gkey, k), in1=r[:, :w],
                                                            op0=ALU.mult, op1=ALU.mult)),
                             reads=[(tag, "xg", b), (tag, "rstd", s), "vec"], writes=[(tag, xkey, s, k)])

        def run(t0, gkey, src=src, widths=None, xn=xn, xkey="xn"):
            stats(t0, src=src, widths=widths)
            apply(t0, gkey, src=src, widths=widths, xn=xn, xkey=xkey)
        run.stats = stats
        run.apply = apply
        cnt = [0]
        return run

    def wload(self, wbuf, slot, src_ap, key, sem):
        self.S.dma("pool", wbuf[slot][:].rearrange("p k c -> p (k c)"), src_ap,
                   writes=[(key, slot)], sem=f"{sem}{slot}")

    @staticmethod
    def ffn_jobs(lo, hi, unit=64):
        n = (hi - lo) // unit
        assert (hi - lo) % unit == 0
        per = 2 * TS // unit
        njobs = -(-n // per)
        base, extra = divmod(n, njobs)
        jobs = []
        t = lo
        for i in range(njobs):
            u = base + (1 if i < extra else 0)
            u0 = (u + 1) // 2
            w = [u0 * unit, (u - u0) * unit]
            w = [x for x in w if x > 0]
            jobs.append((t, w))
            t += u * unit
        assert t == hi
        return jobs

    def ph_ffn(self, l, f, rng, src):
        nc, S = self.nc, self.S
        lf = l * 2 + f
        lo, hi = rng
        with ExitStack() as st:
            xn = [st.enter_context(nc.sbuf_tensor(f"f_xn{s}", [P, KD, TS], BF16)) for s in range(2)]
            h = [st.enter_context(nc.sbuf_tensor(f"f_h{s}", [P, KF, TS], BF16)) for s in range(2)]
            NWG = 3
            wgb = [st.enter_context(nc.sbuf_tensor(f"f_wg{i}", [P, KD, P], BF16)) for i in range(NWG)]
            wub = [st.enter_context(nc.sbuf_tensor(f"f_wu{i}", [P, KD, P], BF16)) for i in range(NWG)]
            wdb = [st.enter_context(nc.sbuf_tensor(f"f_wd{i}", [P, KF, P], BF16)) for i in range(2)]
            sg = [st.enter_context(nc.sbuf_tensor(f"f_sg{i}", [P, TS], BF16)) for i in range(2)]
            xr = [st.enter_context(nc.sbuf_tensor(f"f_xr{i}", [P, TS], F32)) for i in range(2)]
            yo = [st.enter_context(nc.sbuf_tensor(f"f_yo{i}", [P, TS], F32)) for i in range(2)]
            ps_stat = st.enter_context(nc.psum_tensor("f_pstat", [P, TS], F32))
            pg = [st.enter_context(nc.psum_tensor(f"f_pg{i}", [P, TS], F32)) for i in range(2)]
            pu = [st.enter_context(nc.psum_tensor(f"f_pu{i}", [P, TS], F32)) for i in range(2)]
            pd = [st.enter_context(nc.psum_tensor(f"f_pd{i}", [P, TS], F32)) for i in range(2)]
            norm = self.emit_norm(st, src, 0, 2, None, xn, ps_stat, "fn")
            srcv = self.xview(src)
            dstv = self.xview(self.xs)
            wcnt = 0
            dcnt = 0
            rcnt = 0
            jobs = self.ffn_jobs(lo, hi)
            norm(jobs[0][0], ("ffn_norm", l, f), widths=jobs[0][1])
            for ji, (t0, widths) in enumerate(jobs):
                offs = [sum(widths[:i]) for i in range(len(widths))]
                ns = len(widths)
                nxt = jobs[ji + 1] if ji + 1 < len(jobs) else None
                for j in range(KF):
                    if j == KF - 12 and nxt is not None:
                        norm.stats(nxt[0], widths=nxt[1])
                    slot = wcnt % NWG
                    wcnt += 1
                    self.wload(wgb, slot, self.wg[lf, j], "wg", "fwg")
                    self.wload(wub, slot, self.wu[lf, j], "wu", "fwu")
                    for s in range(ns):
                        w = widths[s]
                        for k in range(KD):
                            S.op("pe", (lambda e, o=pg[s], w_=wgb[slot], x=xn[s], k=k, w=w:
                                        e.matmul(o[:, :w], w_[:, k, :], x[:, k, :w], start=(k == 0), stop=(k == KD - 1))),
                                 reads=[("wg", slot), ("fn", "xn", s, k)], writes=[("pg", s)])
                        for k in range(KD):
                            S.op("pe", (lambda e, o=pu[s], w_=wub[slot], x=xn[s], k=k, w=w:
                                        e.matmul(o[:, :w], w_[:, k, :], x[:, k, :w], start=(k == 0), stop=(k == KD - 1))),
                                 reads=[("wu", slot), ("fn", "xn", s, k)], writes=[("pu", s)])
                        S.op("act", (lambda e, o=sg[s], i=pg[s], w=w:
                                     e.activation(out=o[:, :w], in_=i[:, :w], func=AF.Silu)),
                             reads=[("pg", s)], writes=[("sg", s)])
                        S.op("dve", (lambda e, o=h[s], a_=sg[s], b_=pu[s], j=j, w=w:
                                     e.tensor_tensor(o[:, j, :w], a_[:, :w], b_[:, :w], ALU.mult)),
                             reads=[("sg", s), ("pu", s)], writes=[("h", s, j)])
                if nxt is not None:
                    norm.apply(nxt[0], ("ffn_norm", l, f), widths=nxt[1])
                for o_ in range(KD):
                    slot = dcnt % 2
                    dcnt += 1
                    self.wload(wdb, slot, self.wd[lf, o_], "wd", "fwd")
                    for s in range(ns):
                        w = widths[s]
                        ts0 = t0 + offs[s]
                        rb = rcnt % 2
                        rcnt += 1
                        S.dma("sp", xr[rb][:, :w], srcv[:, o_, ts0:ts0 + w], writes=[("xr", rb)], sem=f"fxr{rb}")
                        for j in range(KF):
                            S.op("pe", (lambda e, o=pd[s], w_=wdb[slot], x=h[s], j=j, w=w:
                                        e.matmul(o[:, :w], w_[:, j, :], x[:, j, :w], start=(j == 0), stop=(j == KF - 1))),
                                 reads=[("wd", slot), ("h", s, j)], writes=[("pd", s)])
                        S.op("dve", (lambda e, o=yo[rb], i=pd[s], x=xr[rb], w=w:
                                     e.scalar_tensor_tensor(out=o[:, :w], in0=i[:, :w], scalar=0.5, in1=x[:, :w],
                                                            op0=ALU.mult, op1=ALU.add)),
                             reads=[("pd", s), ("xr", rb)], writes=[("yo", rb)])
                        S.dma("sp", dstv[:, o_, ts0:ts0 + w], yo[rb][:, :w], reads=[("yo", rb)], sem=f"fyo{rb}")
            S.emit()

    def ph_qkv(self, l, j, rng):
        nc, S = self.nc, self.S
        a, b = rng
        assert (b - a) % 2 == 0
        with ExitStack() as st:
            xns = [[st.enter_context(nc.sbuf_tensor(f"q_xn{b_}{s}", [P, KD, TS], BF16)) for s in range(2)] for b_ in range(2)]
            xn = xns[0]
            NW = 3
            wb = [st.enter_context(nc.sbuf_tensor(f"q_w{i}", [P, KD, P], BF16)) for i in range(NW)]
            wvb = [st.enter_context(nc.sbuf_tensor(f"q_wv{i}", [P, KD, 512], BF16)) for i in range(2)]
            cs = st.enter_context(nc.sbuf_tensor("q_cos", [32, 2 * TS], F32))
            sn = st.enter_context(nc.sbuf_tensor("q_sin", [32, 2 * TS], F32))
            NQ = 4
            qb = [st.enter_context(nc.sbuf_tensor(f"q_qb{i}", [P, TS], BF16)) for i in range(NQ)]
            t1 = [st.enter_context(nc.sbuf_tensor(f"q_t1{i}", [32, TS], F32)) for i in range(3)]
            t2 = [st.enter_context(nc.sbuf_tensor(f"q_t2{i}", [32, TS], F32)) for i in range(3)]
            pend = [None]
            vb = [st.enter_context(nc.sbuf_tensor(f"q_vb{i}", [P, 512], BF16)) for i in range(3)]
            ps_stat = st.enter_context(nc.psum_tensor("q_pstat", [P, TS], F32))
            pq = [st.enter_context(nc.psum_tensor(f"q_pq{i}", [P, TS], F32)) for i in range(2)]
            pp = [st.enter_context(nc.psum_tensor(f"q_pp{i}", [P, TS], F32)) for i in range(2)]
            pv = [st.enter_context(nc.psum_tensor(f"q_pv{i}", [P, 512], F32)) for i in range(2)]
            norm = self.emit_norm(st, self.xs, 0, 2, None, xn, ps_stat, "qn")
            wcnt = 0
            qcnt = 0
            vcnt = 0
            wvcnt = 0
            npair = (b - a) // 2
            norm(a * TS, ("mix_norm", l), xn=xns[0], xkey="xn0")
            for pi in range(npair):
                t0 = (a + 2 * pi) * TS
                xn = xns[pi % 2]
                xk = f"xn{pi % 2}"
                S.dma("sp", cs[:], self.cosT[:, t0:t0 + 2 * TS], writes=["cos"], sem="qcs")
                S.dma("sp", sn[:], self.sinT[:, t0:t0 + 2 * TS], writes=["sin"], sem="qcs")
                for c in range(24):
                    if c == 8 and pi + 1 < npair:
                        norm(t0 + 2 * TS, ("mix_norm", l), xn=xns[(pi + 1) % 2], xkey=f"xn{(pi + 1) % 2}")
                    slot = wcnt % NW
                    wcnt += 1
                    self.wload(wb, slot, self.wqk[j, c], "w", "qw")
                    dst = self.qs if c < 12 else self.ks
                    hd = c % 12
                    for s in range(2):
                        ts0 = t0 + s * TS
                        for k in range(KD):
                            S.op("pe", (lambda e, o=pq[s], w=wb[slot], x=xn[s], k=k:
                                        e.matmul(o[:], w[:, k, :], x[:, k, :], start=(k == 0), stop=(k == KD - 1))),
                                 reads=[("w", slot), ("qn", xk, s, k)], writes=[("pq", s)])
                        qi = qcnt % NQ
                        ti = qcnt % 3
                        qcnt += 1
                        S.op("act", (lambda e, o=qb[qi], i=pq[s]: e.copy(o[:], i[:])),
                             reads=[("pq", s)], writes=[("qb", qi)])
                        S.op("dve", (lambda e, o=t2[ti], i=pq[s], s=s:
                                     e.tensor_tensor(o[:], i[0:32, :], cs[:, s * TS:(s + 1) * TS], ALU.mult)),
                             reads=[("pq", s), "cos", ("qb", qi)], writes=[("t2", ti)])
                        if pend[0] is not None:
                            pend[0]()

                        def fin(qi=qi, ti=ti, s=s, ts0=ts0, dst=dst, hd=hd, pb_=qcnt % 2):
                            S.op("pe", (lambda e, o=pp[pb_], i=qb[qi]:
                                        e.matmul(o[0:32, :], self.perm[0:32, 0:32], i[0:32, :], start=True, stop=True)),
                                 reads=[("qb", qi), "perm"], writes=[("pp", pb_)])
                            S.op("dve", (lambda e, o=t1[ti], i=pp[pb_], s=s:
                                         e.tensor_tensor(o[:], i[0:32, :], sn[:, s * TS:(s + 1) * TS], ALU.mult)),
                                 reads=[("pp", pb_), "sin"], writes=[("t1", ti)])
                            S.op("dve", (lambda e, o=qb[qi], x=t1[ti], y=t2[ti]:
                                         e.tensor_tensor(o[0:32, :], x[:], y[:], ALU.add)),
                                 reads=[("t1", ti), ("t2", ti), ("qb", qi)], writes=[("qb", qi)])
                            S.dma("sp", dst[hd * P:(hd + 1) * P, ts0:ts0 + TS], qb[qi][:], reads=[("qb", qi)], sem=f"qst{qi}")
                        pend[0] = fin
                if pend[0] is not None:
                    pend[0]()
                    pend[0] = None
                for g in range(3):
                    slot = wvcnt % 2
                    wvcnt += 1
                    S.dma("pool", wvb[slot][:].rearrange("p k c -> p (k c)"), self.wv[j, g],
                          writes=[("wv", slot)], sem=f"qwv{slot}")
                    for s in range(2):
                        for tb in range(4):
                            pb = vcnt % 2
                            vi = vcnt % 3
                            vcnt += 1
                            for k in range(KD):
                                S.op("pe", (lambda e, o=pv[pb], w=wvb[slot], x=xn[s], k=k, tb=tb:
                                            e.matmul(o[:], x[:, k, tb * P:(tb + 1) * P], w[:, k, :],
                                                     start=(k == 0), stop=(k == KD - 1))),
                                     reads=[("wv", slot), ("qn", xk, s, k)], writes=[("pv", pb)])
                            S.op("act", (lambda e, o=vb[vi], i=pv[pb]: e.copy(o[:], i[:])),
                                 reads=[("pv", pb)], writes=[("vb", vi)])
                            r0 = t0 + s * TS + tb * P
                            S.dma("sp", self.vs[r0:r0 + P, g * 512:(g + 1) * 512], vb[vi][:],
                                  reads=[("vb", vi)], sem=f"qvs{vi}")
            S.emit()

    def ph_att(self, rng):
        nc, S = self.nc, self.S
        scale = 1.0 / math.sqrt(128.0)
        AT = 4 * TS

        def sl(base, n, step):
            return slice(base, base + (n - 1) * step + 1, step)

        with ExitStack() as st:
            qt = [st.enter_context(nc.sbuf_tensor(f"a_qt{i}", [P, AT], BF16)) for i in range(2)]
            kt = [st.enter_context(nc.sbuf_tensor(f"a_kt{i}", [P, 2 * AT], BF16)) for i in range(2)]
            vt = [st.enter_context(nc.sbuf_tensor(f"a_vt{i}", [P, 32, P], BF16)) for i in range(2)]
            oq = st.enter_context(nc.sbuf_tensor("a_oq", [3, AT], BF16))
            ok_ = st.enter_context(nc.sbuf_tensor("a_ok", [3, 2 * AT], BF16))
            nd = st.enter_context(nc.sbuf_tensor("a_nd", [P, 3, 2, AT], F32))
            dt_ = st.enter_context(nc.sbuf_tensor("a_dt", [P, AT], F32))
            ob = [st.enter_context(nc.sbuf_tensor(f"a_ob{i}", [P, AT], BF16)) for i in range(2)]
            pt = [st.enter_context(nc.sbuf_tensor(f"a_pt{i}", [P, 256], BF16)) for i in range(3)]
            negb = st.enter_context(nc.sbuf_tensor("a_negb", [P, 1], F32))
            ps_s = [st.enter_context(nc.psum_tensor(f"a_ps{i}", [P, 256], F32)) for i in range(3)]
            ps_n = [st.enter_context(nc.psum_tensor(f"a_pn{i}", [P, 2, P], F32)) for i in range(3)]
            S.op("pool", (lambda e: e.memset(negb[:], -BIG * scale)), writes=["negb"])
            pend = [None]

            def pv_part(pi_, hb, r, m, nb, g, qc):
                bA = r * nb + m
                S.op("pe", (lambda e, o=ps_n[pi_], v=vt[hb], p_=pt[pi_], bA=bA:
                            e.matmul(o[:, 0, :], v[:, bA, :], p_[:, 0:P], start=True, stop=False)),
                     reads=[("vt", hb, r), ("pt", pi_)], writes=[("psn", pi_)])
                S.op("pe", (lambda e, o=ps_n[pi_], v=vt[hb], p_=pt[pi_], bA=bA:
                            e.matmul(o[:, 0, :], v[:, bA + 1, :], p_[:, P:2 * P], start=False, stop=True)),
                     reads=[("vt", hb, r), ("pt", pi_)], writes=[("psn", pi_)])
                S.op("pe", (lambda e, o=ps_n[pi_], p_=pt[pi_]:
                            e.matmul(o[:, 1, :], self.ones[:], p_[:, 0:P], start=True, stop=False)),
                     reads=["ones", ("pt", pi_)], writes=[("psn", pi_)])
                S.op("pe", (lambda e, o=ps_n[pi_], p_=pt[pi_]:
                            e.matmul(o[:, 1, :], self.ones[:], p_[:, P:2 * P], start=False, stop=True)),
                     reads=["ones", ("pt", pi_)], writes=[("psn", pi_)])
                S.op("dve", (lambda e, i=ps_n[pi_], g=g, qc=qc:
                             e.tensor_copy(nd[:, g, :, qc], i[:])),
                     reads=[("psn", pi_)], writes=[("nd", g)])

            hcnt = 0
            qbc = 0
            ocnt = 0
            for a0 in att_tiles(rng):
                T0 = a0 * TS
                S.dma("pool", oq[:], self.ohq[:, T0:T0 + AT], writes=["oq"], sem="aoq")
                S.dma("pool", ok_[:], self.ohk[:, T0 - 1024:T0 + AT + 1024], writes=["ok"], sem="aoq")
                for h in range(4):
                    for g in range(3):
                        hd = g * 4 + h
                        d = DIL[g]
                        halo = 64 * d
                        nqb = 16 // d
                        nb = nqb + 1
                        hb = hcnt % 2
                        hcnt += 1
                        S.dma("sp", qt[hb][:], self.qs[hd * P:(hd + 1) * P, T0:T0 + AT], writes=[("qt", hb)], sem=f"aq{hb}")
                        S.dma("sp", kt[hb][:, 0:AT + 2 * halo], self.ks[hd * P:(hd + 1) * P, T0 - halo:T0 + AT + halo],
                              writes=[("kt", hb)], sem=f"ak{hb}")
                        for r in range(d):
                            start = T0 - halo + r
                            src = self.vs[sl(start, P * nb, d), hd * P:(hd + 1) * P].rearrange("(b j) c -> j b c", j=P)
                            S.dma("sp", vt[hb][:, r * nb:(r + 1) * nb, :], src, writes=[("vt", hb, r)], sem=f"av{hb}")
                        for r in range(d):
                            for m in range(nqb):
                                qc = sl(P * m * d + r, P, d)
                                kA = sl(P * m * d + r, P, d)
                                kB = sl(P * (m + 1) * d + r, P, d)
                                off = 1024 - halo
                                oA = sl(P * m * d + r + off, P, d)
                                oB = sl(P * (m + 1) * d + r + off, P, d)
                                pi_ = qbc % 3
                                qbc += 1
                                for half, kc, oc in ((0, kA, oA), (1, kB, oB)):
                                    o_ap = (lambda half=half, pi_=pi_: ps_s[pi_][:, half * P:(half + 1) * P])
                                    S.op("pe", (lambda e, oa=o_ap, kc=kc, qc=qc, hb=hb:
                                                e.matmul(oa(), kt[hb][:, kc], qt[hb][:, qc], start=True, stop=False)),
                                         reads=[("kt", hb), ("qt", hb)], writes=[("pss", pi_)])
                                    S.op("pe", (lambda e, oa=o_ap, oc=oc, qc=qc:
                                                e.matmul(oa(), ok_[0:3, oc], oq[0:3, qc], start=False, stop=False)),
                                         reads=["ok", "oq"], writes=[("pss", pi_)])
                                    S.op("pe", (lambda e, oa=o_ap, half=half:
                                                e.matmul(oa(), self.ident[:], self.maskb[:, half * P:(half + 1) * P],
                                                         start=False, stop=True)),
                                         reads=["ident", "maskb"], writes=[("pss", pi_)])
                                S.op("act", (lambda e, o=pt[pi_], i=ps_s[pi_]:
                                             e.activation(out=o[:], in_=i[:], func=AF.Exp, bias=negb[:, 0:1], scale=scale)),
                                     reads=[("pss", pi_), "negb"], writes=[("pt", pi_)])
                                if pend[0] is not None:
                                    pend[0]()
                                pend[0] = (lambda pi_=pi_, hb=hb, r=r, m=m, nb=nb, g=g, qc=qc: pv_part(pi_, hb, r, m, nb, g, qc))
                    if pend[0] is not None:
                        pend[0]()
                        pend[0] = None
                    S.op("dve", (lambda e: e.tensor_tensor(dt_[:], nd[:, 0, 1, :], nd[:, 1, 1, :], ALU.add)),
                         reads=[("nd", 0), ("nd", 1)], writes=["dt"])
                    S.op("pool", (lambda e: e.tensor_tensor(dt_[:], dt_[:], nd[:, 2, 1, :], ALU.add)),
                         reads=["dt", ("nd", 2)], writes=["dt"])
                    S.op("dve", (lambda e: e.reciprocal(dt_[:], dt_[:])), reads=["dt"], writes=["dt"])
                    for g in range(3):
                        hd = g * 4 + h
                        oi = ocnt % 2
                        ocnt += 1
                        eng = "dve" if g != 1 else "pool"
                        S.op(eng, (lambda e, o=ob[oi], g=g: e.tensor_tensor(o[:], nd[:, g, 0, :], dt_[:], ALU.mult)),
                             reads=[("nd", g), "dt"], writes=[("ob", oi)])
                        S.dma("sp", self.os_[hd * P:(hd + 1) * P, T0:T0 + AT], ob[oi][:], reads=[("ob", oi)], sem=f"ao{oi}")
            S.emit()

    def emit_proj_residual(self, st, tag, in_tiles, nk, w_dram, bias_key, t0, state):
        nc, S = self.nc, self.S
        if "wb" not in state:
            state["wb"] = [st.enter_context(nc.sbuf_tensor(f"{tag}_w{i}", [P, nk, P], BF16)) for i in range(3)]
            state["xr"] = [st.enter_context(nc.sbuf_tensor(f"{tag}_xr{i}", [P, TS], F32)) for i in range(2)]
            state["yo"] = [st.enter_context(nc.sbuf_tensor(f"{tag}_yo{i}", [P, TS], F32)) for i in range(2)]
            state["pd"] = [st.enter_context(nc.psum_tensor(f"{tag}_pd{i}", [P, TS], F32)) for i in range(2)]
            state["wc"] = 0
            state["rc"] = 0
        wb, xr, yo, pd = state["wb"], state["xr"], state["yo"], state["pd"]
        xv = self.xview(self.xs)
        for o_ in range(KD):
            slot = state["wc"] % 3
            state["wc"] += 1
            self.wload(wb, slot, w_dram[o_], (tag, "w"), f"{tag}w")
            for s in range(len(in_tiles)):
                ts0 = t0 + s * TS
                rb = state["rc"] % 2
                state["rc"] += 1
                S.dma("sp", xr[rb][:], xv[:, o_, ts0:ts0 + TS], writes=[(tag, "xr", rb)], sem=f"{tag}xr{rb}")
                for k in range(nk):
                    S.op("pe", (lambda e, o=pd[s], w=wb[slot], x=in_tiles[s], k=k:
                                e.matmul(o[:], w[:, k, :], x[:, k, :], start=(k == 0), stop=(k == nk - 1))),
                         reads=[((tag, "w"), slot), (tag, "in", s, k)], writes=[(tag, "pd", s)])
                if bias_key is None:
                    S.op("dve", (lambda e, o=yo[rb], i=pd[s], x=xr[rb]:
                                 e.tensor_tensor(o[:], i[:], x[:], ALU.add)),
                         reads=[(tag, "pd", s), (tag, "xr", rb)], writes=[(tag, "yo", rb)])
                else:
                    S.op("dve", (lambda e, o=yo[rb], i=pd[s], x=xr[rb], o_=o_:
                                 e.scalar_tensor_tensor(out=o[:], in0=i[:], scalar=self.vcol(bias_key, o_), in1=x[:],
                                                        op0=ALU.add, op1=ALU.add)),
                         reads=[(tag, "pd", s), (tag, "xr", rb), "vec"], writes=[(tag, "yo", rb)])
                S.dma("sp", xv[:, o_, ts0:ts0 + TS], yo[rb][:], reads=[(tag, "yo", rb)], sem=f"{tag}yo{rb}")

    def ph_wo(self, j, rng):
        nc, S = self.nc, self.S
        a, b = rng
        assert (b - a) % 2 == 0
        with ExitStack() as st:
            ot = [st.enter_context(nc.sbuf_tensor(f"o_ot{s}", [P, NHEAD, TS], BF16)) for s in range(2)]
            ov = self.os_.rearrange("(k p) t -> p k t", p=P)
            state = {}
            for pi in range((b - a) // 2):
                t0 = (a + 2 * pi) * TS
                for s in range(2):
                    ts0 = t0 + s * TS
                    S.dma("sp", ot[s][:], ov[:, :, ts0:ts0 + TS],
                          writes=[("wo", "in", s, k) for k in range(NHEAD)], sem=f"oot{s}")
                self.emit_proj_residual(st, "wo", ot, NHEAD, self.wo[j], None, t0, state)
            S.emit()

    def ph_c1(self, l, j, rng):
        nc, S = self.nc, self.S
        a, b = rng
        assert (b - a) % 2 == 0
        with ExitStack() as st:
            xns = [[st.enter_context(nc.sbuf_tensor(f"c_xn{b_}{s}", [P, KD, TS], BF16)) for s in range(2)] for b_ in range(2)]
            xn = xns[0]
            NW = 3
            wa = [st.enter_context(nc.sbuf_tensor(f"c_wa{i}", [P, KD, P], BF16)) for i in range(NW)]
            wb_ = [st.enter_context(nc.sbuf_tensor(f"c_wb{i}", [P, KD, P], BF16)) for i in range(NW)]
            sb = [st.enter_context(nc.sbuf_tensor(f"c_sb{i}", [P, TS], F32)) for i in range(2)]
            gl = [st.enter_context(nc.sbuf_tensor(f"c_gl{i}", [P, TS], BF16)) for i in range(3)]
            ps_stat = st.enter_context(nc.psum_tensor("c_pstat", [P, TS], F32))
            pa = [st.enter_context(nc.psum_tensor(f"c_pa{i}", [P, TS], F32)) for i in range(2)]
            pb = [st.enter_context(nc.psum_tensor(f"c_pb{i}", [P, TS], F32)) for i in range(2)]
            norm = self.emit_norm(st, self.xs, 0, 2, None, xn, ps_stat, "cn")
            gv = self.xview(self.gs)
            wcnt = 0
            gcnt = 0
            npair = (b - a) // 2
            norm(a * TS, ("mix_norm", l), xn=xns[0], xkey="xn0")
            for pi in range(npair):
                t0 = (a + 2 * pi) * TS
                xn = xns[pi % 2]
                xk = f"xn{pi % 2}"
                for cc in range(KD):
                    if cc == 6 and pi + 1 < npair:
                        norm(t0 + 2 * TS, ("mix_norm", l), xn=xns[(pi + 1) % 2], xkey=f"xn{(pi + 1) % 2}")
                    slot = wcnt % NW
                    wcnt += 1
                    self.wload(wa, slot, self.win[j, cc], "wa", "cwa")
                    self.wload(wb_, slot, self.win[j, KD + cc], "wb", "cwb")
                    for s in range(2):
                        ts0 = t0 + s * TS
                        for k in range(KD):
                            S.op("pe", (lambda e, o=pa[s], w=wa[slot], x=xn[s], k=k:
                                        e.matmul(o[:], w[:, k, :], x[:, k, :], start=(k == 0), stop=(k == KD - 1))),
                                 reads=[("wa", slot), ("cn", xk, s, k)], writes=[("pa", s)])
                        for k in range(KD):
                            S.op("pe", (lambda e, o=pb[s], w=wb_[slot], x=xn[s], k=k:
                                        e.matmul(o[:], w[:, k, :], x[:, k, :], start=(k == 0), stop=(k == KD - 1))),
                                 reads=[("wb", slot), ("cn", xk, s, k)], writes=[("pb", s)])
                        gi = gcnt % 3
                        gcnt += 1
                        S.op("act", (lambda e, o=sb[s], i=pb[s], cc=cc:
                                     e.activation(out=o[:], in_=i[:], func=AF.Sigmoid,
                                                  bias=self.vcol(("b_in", j), KD + cc), scale=1.0)),
                             reads=[("pb", s), "vec"], writes=[("sb", s)])
                        S.op("dve", (lambda e, o=gl[gi], i=pa[s], g_=sb[s], cc=cc:
                                     e.scalar_tensor_tensor(out=o[:], in0=i[:], scalar=self.vcol(("b_in", j), cc),
                                                            in1=g_[:], op0=ALU.add, op1=ALU.mult)),
                             reads=[("pa", s), ("sb", s), "vec"], writes=[("gl", gi)])
                        S.dma("sp", gv[:, cc, ts0:ts0 + TS], gl[gi][:], reads=[("gl", gi)], sem=f"cgl{gi}")
            S.emit()

    def ph_c2(self, j, rng):
        nc, S = self.nc, self.S
        a, b = rng
        assert (b - a) % 2 == 0
        GW = TS + CW - 1
        with ExitStack() as st:
            hcv = [st.enter_context(nc.sbuf_tensor(f"d_hcv{s}", [P, KD, TS], F32)) for s in range(2)]
            hn = [st.enter_context(nc.sbuf_tensor(f"d_hn{s}", [P, KD, TS], BF16)) for s in range(2)]
            gt = [st.enter_context(nc.sbuf_tensor(f"d_gt{i}", [P, GW], BF16)) for i in range(4)]
            dg = [st.enter_context(nc.sbuf_tensor(f"d_dg{i}", [P, CW, P], BF16)) for i in range(3)]
            hb = [st.enter_context(nc.sbuf_tensor(f"d_hb{i}", [P, TS], BF16)) for i in range(2)]
            sq = [st.enter_context(nc.sbuf_tensor(f"d_sq{i}", [P, TS], BF16)) for i in range(2)]
            mean = st.enter_context(nc.sbuf_tensor("d_mean", [P, TS], F32))
            msq = st.enter_context(nc.sbuf_tensor("d_msq", [P, TS], F32))
            rstd = st.enter_context(nc.sbuf_tensor("d_rstd", [P, TS], F32))
            ps_sum = [st.enter_context(nc.psum_tensor(f"d_psum{i}", [P, TS], F32)) for i in range(2)]
            ps_sq = [st.enter_context(nc.psum_tensor(f"d_psq{i}", [P, TS], F32)) for i in range(2)]
            pc = [st.enter_context(nc.psum_tensor(f"d_pc{i}", [P, TS], F32)) for i in range(2)]
            gv = self.xview(self.gs)
            state = {}
            gcnt = 0
            dcnt = 0
            wbase = self.VEC[("w_dw", j)]
            for pi in range((b - a) // 2):
                t0 = (a + 2 * pi) * TS
                for cc in range(KD):
                    di = dcnt % 3
                    dcnt += 1
                    en = "dve"
                    for k in range(CW):
                        c0 = wbase + cc * CW + k
                        S.op(en, (lambda e, o=dg[di], k=k, c0=c0:
                                  e.tensor_scalar(o[:, k, :], self.ident[:], self.vec_sb[:, c0:c0 + 1], None, ALU.mult)),
                             reads=["ident", "vec"], writes=[("dg", di)])
                    for s in range(2):
                        t = a + 2 * pi + s
                        ts0 = t * TS
                        gi = gcnt % 4
                        gcnt += 1
                        S.dma("sp", gt[gi][:], gv[:, cc, ts0 - 15:ts0 - 15 + GW], writes=[("gt", gi)], sem=f"dgt{gi}")
                        S.op("pool", (lambda e, g_=gt[gi], t=t:
                                      e.tensor_scalar(g_[:, 0:15], g_[:, 0:15], self.flag_sb[:, 2 * t:2 * t + 1], None, ALU.mult)),
                             reads=[("gt", gi), "flag"], writes=[("gt", gi)])
                        S.op("pool", (lambda e, g_=gt[gi], t=t:
                                      e.tensor_scalar(g_[:, TS + 15:GW], g_[:, TS + 15:GW],
                                                      self.flag_sb[:, 2 * t + 1:2 * t + 2], None, ALU.mult)),
                             reads=[("gt", gi), "flag"], writes=[("gt", gi)])
                        for k in range(CW):
                            S.op("pe", (lambda e, o=pc[s], w=dg[di], g_=gt[gi], k=k:
                                        e.matmul(o[:], w[:, k, :], g_[:, k:k + TS], start=(k == 0), stop=(k == CW - 1))),
                                 reads=[("dg", di), ("gt", gi)], writes=[("pc", s)])
                        S.op("act", (lambda e, o=hcv[s], i=pc[s], cc=cc:
                                     e.activation(out=o[:, cc, :], in_=i[:], func=AF.Identity,
                                                  bias=self.vcol(("b_dw", j), cc), scale=1.0)),
                             reads=[("pc", s), "vec"], writes=[("hcv", s, cc)])
                        q = gcnt % 2
                        S.op("act", (lambda e, o=hb[q], i=hcv[s], cc=cc: e.copy(o[:], i[:, cc, :])),
                             reads=[("hcv", s, cc)], writes=[("hb", q)])
                        S.op("pe", (lambda e, o=ps_sum[s], i=hb[q], cc=cc:
                                    e.matmul(o[:], self.ones[:], i[:], start=(cc == 0), stop=(cc == KD - 1))),
                             reads=[("hb", q), "ones"], writes=[("psum", s)])
                        S.op("act", (lambda e, o=sq[q], i=hcv[s], cc=cc:
                                     e.activation(out=o[:], in_=i[:, cc, :], func=AF.Square)),
                             reads=[("hcv", s, cc)], writes=[("sq", q)])
                        S.op("pe", (lambda e, o=ps_sq[s], i=sq[q], cc=cc:
                                    e.matmul(o[:], self.ones[:], i[:], start=(cc == 0), stop=(cc == KD - 1))),
                             reads=[("sq", q), "ones"], writes=[("psq", s)])
                for s in range(2):
                    S.op("dve", (lambda e, s=s: e.tensor_scalar(mean[:], ps_sum[s][:], 1.0 / D, None, ALU.mult)),
                         reads=[("psum", s)], writes=["mean"])
                    S.op("dve", (lambda e: e.tensor_tensor(msq[:], mean[:], mean[:], ALU.mult)),
                         reads=["mean"], writes=["msq"])
                    S.op("dve", (lambda e, s=s: e.scalar_tensor_tensor(out=msq[:], in0=ps_sq[s][:], scalar=1.0 / D, in1=msq[:],
                                                                       op0=ALU.mult, op1=ALU.subtract)),
                         reads=[("psq", s), "msq"], writes=["msq"])
                    S.op("act", (lambda e: e.activation(out=rstd[:], in_=msq[:], func=AF.Sqrt,
                                                        bias=self.epsc[:, 1:2], scale=1.0)),
                         reads=["msq", "epsc"], writes=["rstd"])
                    S.op("dve", (lambda e: e.reciprocal(rstd[:], rstd[:])), reads=["rstd"], writes=["rstd"])
                    for cc in range(KD):
                        S.op("dve", (lambda e, o=hcv[s], cc=cc: e.tensor_tensor(o[:, cc, :], o[:, cc, :], mean[:], ALU.subtract)),
                             reads=[("hcv", s, cc), "mean"], writes=[("hcv", s, cc)])
                        S.op("dve", (lambda e, o=hcv[s], cc=cc: e.tensor_tensor(o[:, cc, :], o[:, cc, :], rstd[:], ALU.mult)),
                             reads=[("hcv", s, cc), "rstd"], writes=[("hcv", s, cc)])
                        S.op("act", (lambda e, o=hn[s], i=hcv[s], cc=cc:
                                     e.activation(out=o[:, cc, :], in_=i[:, cc, :], func=AF.Silu,
                                                  bias=self.vcol(("ln_b", j), cc), scale=self.vcol(("ln_g", j), cc))),
                             reads=[("hcv", s, cc), "vec"], writes=[("c2", "in", s, cc)])
                self.emit_proj_residual(st, "c2", hn, KD, self.wout[j], ("b_out", j), t0, state)
            S.emit()

    def ph_copy(self, rng):
        nc, S = self.nc, self.S
        lo, hi = rng
        with ExitStack() as st:
            buf = [st.enter_context(nc.sbuf_tensor(f"cp{i}", [P, KD, TS], F32)) for i in range(2)]
            sv = self.xview(self.xin)
            dv = self.xview(self.xs)
            i = 0
            t = lo
            while t < hi:
                w = min(TS, hi - t)
                S.dma("sp", buf[i][:, :, :w], sv[:, :, t:t + w], writes=[("cp", i)], sem=f"cpl{i}")
                S.dma("sp", dv[:, :, t:t + w], buf[i][:, :, :w], reads=[("cp", i)], sem=f"cps{i}")
                i = 1 - i
                t += w
            S.emit()

    def ph_final(self, rng):
        nc, S = self.nc, self.S
        a, b = rng
        o0 = self.cfg.own[0]
        with ExitStack() as st:
            ps_stat = st.enter_context(nc.psum_tensor("z_pstat", [P, TS], F32))
            rstd = st.enter_context(nc.sbuf_tensor("z_rstd", [P, TS], F32))
            xg = [st.enter_context(nc.sbuf_tensor(f"z_xg{i}", [P, 2, TS], F32)) for i in range(4)]
            sq = [st.enter_context(nc.sbuf_tensor(f"z_sq{i}", [P, TS], BF16)) for i in range(2)]
            yo = [st.enter_context(nc.sbuf_tensor(f"z_yo{i}", [P, 2, TS], F32)) for i in range(3)]
            srcv = self.xview(self.xs)
            outv = self.xview(self.yout)
            cnt = 0
            ycnt = 0
            for t in range(a, b):
                ts0 = t * TS
                for kg in range(KD // 2):
                    bb = cnt % 4
                    cnt += 1
                    S.dma("sp", xg[bb][:], srcv[:, kg * 2:kg * 2 + 2, ts0:ts0 + TS], writes=[("xg", bb)], sem=f"zxg{bb}")
                    for kk in range(2):
                        k = kg * 2 + kk
                        q = k % 2
                        S.op("act", (lambda e, o=sq[q], i=xg[bb], kk=kk:
                                     e.activation(out=o[:], in_=i[:, kk, :], func=AF.Square)),
                             reads=[("xg", bb)], writes=[("sq", q)])
                        S.op("pe", (lambda e, i=sq[q], k=k:
                                    e.matmul(ps_stat[:], self.ones[:], i[:], start=(k == 0), stop=(k == KD - 1))),
                             reads=[("sq", q), "ones"], writes=["pstat"])
                S.op("act", (lambda e: e.activation(out=rstd[:], in_=ps_stat[:], func=AF.Sqrt,
                                                    bias=self.epsc[:, 0:1], scale=1.0 / D)),
                     reads=["pstat", "epsc"], writes=["rstd"])
                S.op("dve", (lambda e: e.reciprocal(rstd[:], rstd[:])),
                     reads=["rstd"], writes=["rstd"])
                for kg in range(KD // 2):
                    bb = cnt % 4
                    cnt += 1
                    S.dma("sp", xg[bb][:], srcv[:, kg * 2:kg * 2 + 2, ts0:ts0 + TS], writes=[("xg", bb)], sem=f"zxg{bb}")
                    yb = ycnt % 3
                    ycnt += 1
                    for kk in range(2):
                        k = kg * 2 + kk
                        S.op("dve", (lambda e, o=yo[yb], i=xg[bb], kk=kk, k=k:
                                     e.scalar_tensor_tensor(out=o[:, kk, :], in0=i[:, kk, :],
                                                            scalar=self.vcol(("final_norm",), k), in1=rstd[:],
                                                            op0=ALU.mult, op1=ALU.mult)),
                             reads=[("xg", bb), "rstd", "vec"], writes=[("yo", yb)])
                    oc = (t - o0) * TS
                    S.dma("sp", outv[:, kg * 2:kg * 2 + 2, oc:oc + TS], yo[yb][:], reads=[("yo", yb)], sem=f"zyo{yb}")
            S.emit()


def _bf16_exact(a):
    return a.astype(np.float32)


def host_consts():
    ident = np.eye(P, dtype=np.float32)
    ones = np.ones((P, P), np.float32)
    perm = np.zeros((P, P), np.float32)
    for i in range(16):
        perm[i + 16, i] = 1.0
        perm[i, i + 16] = 1.0
    mb = np.zeros((P, 256), np.float32)
    j = np.arange(P)[:, None]
    i = np.arange(P)[None, :]
    mb[:, 0:128] = np.where(j >= i, 0.0, -BIG)
    mb[:, 128:256] = np.where(j <= i, 0.0, -BIG)
    return np.concatenate([ident, ones, perm, mb], axis=1)


def pk(v):
    v = np.asarray(v, np.float32)
    return np.ascontiguousarray(v.reshape(-1, P).T)


def host_vecs(inp):
    n = Builder.n_vec_cols()
    V = Builder.VEC
    t = np.zeros((P, n), np.float32)

    def put(key, arr):
        t[:, V[key]:V[key] + arr.shape[1]] = arr
    for l in range(DEPTH):
        for f in range(2):
            put(("ffn_norm", l, f), pk(inp["ffn_norm"][l, f]))
        put(("mix_norm", l), pk(inp["mix_norm"][l]))
    for j in range(2):
        put(("b_in", j), pk(inp["conv_b_in"][j]))
        wdw = np.asarray(inp["conv_w_dw"][j], np.float32)
        put(("w_dw", j), np.ascontiguousarray(wdw.reshape(CW, KD, P).transpose(2, 1, 0)).reshape(P, KD * CW))
        put(("b_dw", j), pk(inp["conv_b_dw"][j]))
        put(("ln_g", j), pk(inp["conv_ln_g"][j]))
        put(("ln_b", j), pk(inp["conv_ln_b"][j]))
        put(("b_out", j), pk(inp["conv_b_out"][j]))
    put(("final_norm",), pk(inp["final_norm"]))
    return t


def relayout_w(w, nout_chunk=P):
    w = np.asarray(w, np.float32)
    K, N = w.shape
    a = w.reshape(K // P, P, N // nout_chunk, nout_chunk).transpose(2, 1, 0, 3)
    return np.ascontiguousarray(a).reshape(N // nout_chunk, P, (K // P) * nout_chunk)


def host_weights(inp):
    out = {}
    out["wg"] = np.stack([relayout_w(inp["ffn_w_gate"][l, f]) for l in range(DEPTH) for f in range(2)])
    out["wu"] = np.stack([relayout_w(inp["ffn_w_up"][l, f]) for l in range(DEPTH) for f in range(2)])
    out["wd"] = np.stack([relayout_w(inp["ffn_w_down"][l, f]) for l in range(DEPTH) for f in range(2)])
    wqkv = np.asarray(inp["attn_w_qkv"], np.float32)
    out["wqk"] = np.stack([relayout_w(wqkv[j][:, :2 * AW]) for j in range(2)])
    out["wv"] = np.stack([relayout_w(wqkv[j][:, 2 * AW:], 512) for j in range(2)])
    out["wo"] = np.stack([relayout_w(inp["attn_w_o"][j]) for j in range(2)])
    out["win"] = np.stack([relayout_w(inp["conv_w_in"][j]) for j in range(2)])
    out["wout"] = np.stack([relayout_w(inp["conv_w_out"][j]) for j in range(2)])
    return out


def core_tables(c, nt=NT, halo=HALO):
    ll = nt * TS
    g = OWN * c - halo + np.arange(ll)
    valid = (g >= 0) & (g < NSEQ * SEQ)
    seg = np.where(valid, g // SEQ, -1)
    pos = np.where(valid, g % SEQ, 0).astype(np.float32)
    freqs = (500000.0 ** (-(np.arange(0, 32, 2, dtype=np.float32) / np.float32(32.0)))).astype(np.float32)
    ang = (pos[None, :] * freqs[:, None]).astype(np.float32)
    cos = np.cos(ang).astype(np.float32)
    sin = np.sin(ang).astype(np.float32)
    cosT = np.concatenate([cos, cos], axis=0)
    sinT = np.concatenate([-sin, sin], axis=0)
    segs = [s for s in np.unique(seg) if s >= 0]
    cls = np.full(ll, 2)
    for i, s in enumerate(segs[:2]):
        cls[seg == s] = i
    assert len(segs) <= 2
    oh = np.zeros((3, ll), np.float32)
    oh[cls, np.arange(ll)] = 1.0
    tcls = cls.reshape(nt, TS)[:, 0]
    fl = np.zeros((nt, 2), np.float32)
    for t in range(nt):
        fl[t, 0] = 1.0 if (t > 0 and tcls[t - 1] == tcls[t]) else 0.0
        fl[t, 1] = 1.0 if (t < nt - 1 and tcls[t + 1] == tcls[t]) else 0.0
    flags = np.broadcast_to(fl.reshape(1, nt * 2), (P, nt * 2)).copy()
    return dict(cosT=cosT, sinT=sinT, ohk=oh, ohq=oh * BIG, flags=flags, g=g, valid=valid)


def core_x(xflat, c, nt=NT, halo=HALO):
    ll = nt * TS
    g0 = OWN * c - halo
    lo = max(g0, 0)
    hi = min(g0 + ll, xflat.shape[0])
    out = np.zeros((D, ll), np.float32)
    out[:, lo - g0:hi - g0] = xflat[lo:hi].T
    return out


_CACHE = {}


def kernel(**inputs):
    inp = {k: np.asarray(v) for k, v in inputs.items()}
    xflat = np.concatenate([inp["x_prompt"].reshape(-1, D), inp["x_sample"].reshape(-1, D)], axis=0)
    cfg = Cfg()
    nc = Builder(cfg).build()
    W = host_weights(inp)
    vecs = host_vecs(inp)
    consts = host_consts()
    in_maps = []
    for c in range(NCORES):
        tb = core_tables(c)
        m = dict(W)
        m.update(xin=core_x(xflat, c), vecs=vecs, consts=consts, cosT=tb["cosT"], sinT=tb["sinT"],
                 ohk=tb["ohk"], ohq=tb["ohq"], flags=tb["flags"])
        in_maps.append(m)
    res = run_bass_kernel_spmd(nc, in_maps, core_ids=list(range(NCORES)))
    y = np.concatenate([np.asarray(r["yT"]).T for r in res.results], axis=0)
    y = np.ascontiguousarray(y, dtype=np.float32)
    yp = y[:2 * SEQ].reshape(2, SEQ, D)
    ysm = y[2 * SEQ:].reshape(1, SEQ, D)
    return (yp, ysm)
```

```python
import math
from contextlib import ExitStack

import numpy as np
import concourse.bass as bass
import concourse.mybir as mybir
from concourse.bass_utils import run_bass_kernel_spmd

F32 = mybir.dt.float32
BF16 = mybir.dt.bfloat16
AF = mybir.ActivationFunctionType
ALU = mybir.AluOpType

P = 128
D = 2048
KD = 16
DFF = 5632
KF = 44
TS = 512
NHEAD = 12
AW = 1536
CW = 31
DEPTH = 4
SEQ = 16384
NSEQ = 3
NCORES = 8
OWN = 6144
HALO = 2560
NT = (OWN + 2 * HALO) // TS
LL = NT * TS
OWN_T0 = HALO // TS
OWN_T1 = OWN_T0 + OWN // TS
BIG = 2048.0
RMS_EPS = 1e-6
LN_EPS = 1e-5
DIL = (1, 4, 16)
ENGS = ("pe", "act", "dve", "pool", "sp")


class Op:
    __slots__ = ("eng", "fn", "waits", "is_dma", "sem", "semval", "need_sig", "sig", "tag")

    def __init__(self, eng, fn, is_dma=False):
        self.eng = eng
        self.fn = fn
        self.waits = []
        self.is_dma = is_dma
        self.sem = None
        self.semval = 0
        self.need_sig = False
        self.sig = 0
        self.tag = None


class Sched:
    def __init__(self, nc, stack):
        self.nc = nc
        self.esem = {e: stack.enter_context(nc.semaphore("sg_" + e)) for e in ENGS}
        self.ecount = {e: 0 for e in ENGS}
        self.dsem = {}
        self.dcount = {}
        self.stack = stack
        self.begin()

    def dma_sem(self, name):
        if name not in self.dsem:
            self.dsem[name] = self.stack.enter_context(self.nc.semaphore("dm_" + name))
            self.dcount[name] = 0
        return name

    def begin(self):
        self.q = {e: [] for e in ENGS}
        self.res = {}

    def _deps(self, o, reads, writes):
        res = self.res
        deps = []
        for k in reads:
            st = res.get(k)
            if st is not None and st[0] is not None:
                deps.append((st[0], "raw"))
        for k in writes:
            st = res.get(k)
            if st is not None:
                if st[0] is not None:
                    deps.append((st[0], "waw"))
                for r in st[1]:
                    deps.append((r, "war"))
        for k in reads:
            st = res.get(k)
            if st is None:
                res[k] = [None, [o]]
            else:
                st[1].append(o)
        for k in writes:
            res[k] = [o, []]
        for d, kind in deps:
            if d is o:
                continue
            if not d.is_dma and d.eng == o.eng and not o.is_dma:
                if o.eng == "pe":
                    continue
                if kind != "raw":
                    continue
            o.waits.append(d)
            if not d.is_dma:
                d.need_sig = True

    def op(self, eng, fn, reads=(), writes=()):
        o = Op(eng, fn)
        o.tag = (tuple(writes), tuple(reads))
        self._deps(o, reads, writes)
        self.q[eng].append(o)
        return o

    def dma(self, eng, out, in_, reads=(), writes=(), sem=None):
        name = self.dma_sem(sem)
        o = Op(eng, (lambda e, out=out, in_=in_: e.dma_start(out=out, in_=in_)), is_dma=True)
        o.tag = (tuple(writes), tuple(reads), sem)
        self._deps(o, reads, writes)
        self.dcount[name] += 16
        o.sem = name
        o.semval = self.dcount[name]
        self.q[eng].append(o)
        return o

    def check(self):
        ptr = {e: 0 for e in ENGS}
        done = set()
        total = sum(len(v) for v in self.q.values())
        ndone = 0
        while ndone < total:
            prog = False
            for e in ENGS:
                q = self.q[e]
                while ptr[e] < len(q):
                    o = q[ptr[e]]
                    if all(id(d) in done for d in o.waits):
                        done.add(id(o))
                        ptr[e] += 1
                        ndone += 1
                        prog = True
                    else:
                        break
            if not prog:
                msg = []
                for e in ENGS:
                    if ptr[e] < len(self.q[e]):
                        o = self.q[e][ptr[e]]
                        blk = [(d.eng, d.is_dma, getattr(d, "tag", None)) for d in o.waits if id(d) not in done]
                        msg.append(f"{e}@{ptr[e]} tag={getattr(o, 'tag', None)} blocked on {blk}")
                raise RuntimeError("DEADLOCK in schedule: " + " | ".join(msg))

    def emit(self, final=False):
        nc = self.nc
        self.check()
        for e in ENGS:
            lastc = None
            for o in self.q[e]:
                if not o.is_dma:
                    lastc = o
            if lastc is not None:
                lastc.need_sig = True
            c = self.ecount[e]
            for o in self.q[e]:
                if not o.is_dma and o.need_sig:
                    c += 1
                    o.sig = c
            self.ecount[e] = c
        end_e = dict(self.ecount)
        end_d = dict(self.dcount)
        esem, dsem = self.esem, self.dsem

        def replay(e, eng):
            waited_e = {}
            waited_d = {}
            for o in self.q[e]:
                for d in o.waits:
                    if d.is_dma:
                        if waited_d.get(d.sem, 0) < d.semval:
                            eng.wait_ge(dsem[d.sem], d.semval)
                            waited_d[d.sem] = d.semval
                    else:
                        if waited_e.get(d.eng, 0) < d.sig:
                            eng.wait_ge(esem[d.eng], d.sig)
                            waited_e[d.eng] = d.sig
                ins = o.fn(eng)
                if o.is_dma:
                    ins.then_inc(dsem[o.sem], 16)
                elif o.need_sig:
                    ins.then_inc(esem[e], 1)
            for f in ENGS:
                if f != e and end_e[f] > 0 and waited_e.get(f, 0) < end_e[f]:
                    eng.wait_ge(esem[f], end_e[f])
            for name, v in end_d.items():
                if v > 0 and waited_d.get(name, 0) < v:
                    eng.wait_ge(dsem[name], v)

        with nc.Block() as block:
            @block.tensor
            def _(eng):
                replay("pe", eng)

            @block.scalar
            def _(eng):
                replay("act", eng)

            @block.vector
            def _(eng):
                replay("dve", eng)

            @block.gpsimd
            def _(eng):
                replay("pool", eng)

            @block.sync
            def _(eng):
                replay("sp", eng)
        self.begin()


class Cfg:
    def __init__(self, **kw):
        self.nt = NT
        self.own = (OWN_T0, OWN_T1)
        self.depth = DEPTH
        self.phases = None
        self.dbg = ()
        for k, v in kw.items():
            setattr(self, k, v)


def phase_plan():
    pl = []
    r0 = (0, 22)
    r1 = (2, 20)
    r2 = (4, 18)
    r3 = (5, 17)
    o0, o1 = OWN_T0 * TS, OWN_T1 * TS

    def tk(h):
        return (o0 - h, o1 + h)
    pl += [("copy", 0, 0, (0, o0 - 2112)), ("copy", 0, 0, (o1 + 2112, NT * TS)), ("ffn", 0, 0, tk(2112), "xin"), ("qkv", 0, 0, r0), ("att", 0, 0, r1), ("wo", 0, 0, r1), ("ffn", 0, 1, tk(1088))]
    pl += [("ffn", 1, 0, tk(1088)), ("c1", 1, 0, (1, 21)), ("c2", 1, 0, r1), ("ffn", 1, 1, tk(1088))]
    pl += [("ffn", 2, 0, tk(1088)), ("qkv", 2, 1, r1), ("att", 2, 1, r2), ("wo", 2, 1, r2), ("ffn", 2, 1, tk(64))]
    pl += [("ffn", 3, 0, tk(64)), ("c1", 3, 1, r2), ("c2", 3, 1, r3), ("ffn", 3, 1, tk(0))]
    pl += [("final", 0, 0, r3)]
    return pl


def att_tiles(rng):
    a, b = rng
    offs = list(range(a, b - 3, 4))
    if not offs or offs[-1] + 4 < b:
        offs.append(b - 4)
    return offs


class NcProxy:
    def __init__(self, nc):
        self._nc = nc
        self.uid = 0

    def __getattr__(self, name):
        return getattr(self._nc, name)

    def sbuf_tensor(self, name, shape, dt):
        return self._nc.sbuf_tensor(f"{name}_u{self.uid}", shape, dt)

    def psum_tensor(self, name, shape, dt):
        return self._nc.psum_tensor(f"{name}_u{self.uid}", shape, dt)


class Builder:
    def __init__(self, cfg):
        self.cfg = cfg
        self.nc_real = bass.Bass("TRN2", target_bir_lowering=False)
        self.nc = NcProxy(self.nc_real)
        self.top = ExitStack()

    def declare(self):
        nc = self.nc
        c = self.cfg
        ll = c.nt * TS
        self.ll = ll

        def inp(name, shape, dt=F32):
            return nc.dram_tensor(name, list(shape), dt, kind="ExternalInput").ap()

        def scr(name, shape, dt):
            kind = "ExternalOutput" if name in c.dbg else "Internal"
            return nc.dram_tensor(name, list(shape), dt, kind=kind).ap()

        self.xin = inp("xin", (D, ll))
        self.wg = inp("wg", (DEPTH * 2, KF, P, KD * P))
        self.wu = inp("wu", (DEPTH * 2, KF, P, KD * P))
        self.wd = inp("wd", (DEPTH * 2, KD, P, KF * P))
        self.wqk = inp("wqk", (2, 24, P, KD * P))
        self.wv = inp("wv", (2, 3, P, KD * 512))
        self.wo = inp("wo", (2, KD, P, NHEAD * P))
        self.win = inp("win", (2, 32, P, KD * P))
        self.wout = inp("wout", (2, KD, P, KD * P))
        self.vecs = inp("vecs", (P, self.n_vec_cols()))
        self.cosT = inp("cosT", (32, ll))
        self.sinT = inp("sinT", (32, ll))
        self.ohk = inp("ohk", (3, ll))
        self.ohq = inp("ohq", (3, ll))
        self.flags = inp("flags", (P, c.nt * 2))
        self.consts = inp("consts", (P, 3 * P + 256))
        self.xs = scr("xs", (D, ll), F32)
        self.qs = scr("qs", (AW, ll), BF16)
        self.ks = scr("ks", (AW, ll), BF16)
        self.vs = scr("vs", (ll, AW), BF16)
        self.os_ = scr("os", (AW, ll), BF16)
        self.gs = scr("gs", (D, ll), BF16)
        o0, o1 = c.own
        self.yout = nc.dram_tensor("yT", [D, (o1 - o0) * TS], F32, kind="ExternalOutput").ap()

    VEC = {}

    @classmethod
    def n_vec_cols(cls):
        if not cls.VEC:
            col = 0

            def add(name, n):
                nonlocal col
                cls.VEC[name] = col
                col += n
            for l in range(DEPTH):
                for f in range(2):
                    add(("ffn_norm", l, f), KD)
                add(("mix_norm", l), KD)
            for j in range(2):
                add(("b_in", j), 32)
                add(("w_dw", j), KD * CW)
                add(("b_dw", j), KD)
                add(("ln_g", j), KD)
                add(("ln_b", j), KD)
                add(("b_out", j), KD)
            add(("final_norm",), KD)
            cls.VEC["_n"] = col
        return cls.VEC["_n"]

    def xview(self, t):
        return t.rearrange("(k p) t -> p k t", p=P)

    def build(self):
        nc = self.nc
        c = self.cfg
        self.declare()
        top = self.top
        with top:
            S = self.S = Sched(nc, top)
            self.vec_sb = top.enter_context(nc.sbuf_tensor("vec_sb", [P, self.n_vec_cols()], F32))
            self.ident = top.enter_context(nc.sbuf_tensor("ident", [P, P], BF16))
            self.ones = top.enter_context(nc.sbuf_tensor("ones", [P, P], BF16))
            self.perm = top.enter_context(nc.sbuf_tensor("perm", [P, P], BF16))
            self.maskb = top.enter_context(nc.sbuf_tensor("maskb", [P, 256], BF16))
            self.flag_sb = top.enter_context(nc.sbuf_tensor("flag_sb", [P, c.nt * 2], F32))
            self.epsc = top.enter_context(nc.sbuf_tensor("epsc", [P, 2], F32))
            S.op("pool", (lambda e: e.memset(self.epsc[:, 0:1], RMS_EPS)), writes=["epsc"])
            S.op("pool", (lambda e: e.memset(self.epsc[:, 1:2], LN_EPS)), writes=["epsc"])
            S.dma("sp", self.vec_sb[:], self.vecs[:, :], writes=["vec"], sem="c0")
            S.dma("sp", self.flag_sb[:], self.flags[:, :], writes=["flag"], sem="c0")
            S.dma("pool", self.ident[:], self.consts[:, 0:P], writes=["ident"], sem="c1")
            S.dma("pool", self.ones[:], self.consts[:, P:2 * P], writes=["ones"], sem="c1")
            S.dma("pool", self.perm[:], self.consts[:, 2 * P:3 * P], writes=["perm"], sem="c1")
            S.dma("pool", self.maskb[:], self.consts[:, 3 * P:3 * P + 256], writes=["maskb"], sem="c1")
            S.emit()
            plan = c.phases if c.phases is not None else phase_plan()
            for ph in plan:
                kind = ph[0]
                self.nc.uid += 1
                if kind == "ffn":
                    src = self.xin if (len(ph) > 4 and ph[4] == "xin") else self.xs
                    self.ph_ffn(ph[1], ph[2], ph[3], src)
                elif kind == "qkv":
                    self.ph_qkv(ph[1], ph[2], ph[3])
                elif kind == "att":
                    self.ph_att(ph[3])
                elif kind == "wo":
                    self.ph_wo(ph[2], ph[3])
                elif kind == "c1":
                    self.ph_c1(ph[1], ph[2], ph[3])
                elif kind == "c2":
                    self.ph_c2(ph[2], ph[3])
                elif kind == "final":
                    self.ph_final(ph[3])
                elif kind == "copy":
                    self.ph_copy(ph[3])
                else:
                    raise ValueError(kind)
        return self.nc_real

    def vcol(self, key, k=0, n=1):
        c0 = self.VEC[key] + k
        return self.vec_sb[:, c0:c0 + n]

    def emit_norm(self, st, src, t0, ns, gkey, xn, ps_stat, tag, NXG=2, NSQ=2):
        nc, S = self.nc, self.S
        srcv = self.xview(src)
        XG = 2
        xg = [st.enter_context(nc.sbuf_tensor(f"{tag}_xg{i}", [P, XG, TS], F32)) for i in range(NXG)]
        sq = [st.enter_context(nc.sbuf_tensor(f"{tag}_sq{i}", [P, TS], BF16)) for i in range(NSQ)]
        rstd = [st.enter_context(nc.sbuf_tensor(f"{tag}_rstd{i}", [P, TS], F32)) for i in range(ns)]
        self._norm_bufs = (xg, sq, rstd)

        def stats(t0, src=src, widths=None):
            srcv = self.xview(src)
            if widths is None:
                widths = [TS] * ns
            offs = [sum(widths[:i]) for i in range(len(widths))]
            for s in range(len(widths)):
                ts0 = t0 + offs[s]
                w = widths[s]
                for kg in range(KD // XG):
                    b = cnt[0] % NXG
                    cnt[0] += 1
                    S.dma("sp", xg[b][:, :, :w], srcv[:, kg * XG:(kg + 1) * XG, ts0:ts0 + w],
                          writes=[(tag, "xg", b)], sem=f"{tag}xg{b}")
                    for kk in range(XG):
                        k = kg * XG + kk
                        q = k % 2
                        S.op("act", (lambda e, o=sq[q], i=xg[b], kk=kk, w=w:
                                     e.activation(out=o[:, :w], in_=i[:, kk, :w], func=AF.Square)),
                             reads=[(tag, "xg", b)], writes=[(tag, "sq", q)])
                        S.op("pe", (lambda e, o=ps_stat, i=sq[q], k=k, w=w:
                                    e.matmul(o[:, :w], self.ones[:], i[:, :w], start=(k == 0), stop=(k == KD - 1))),
                             reads=[(tag, "sq", q), "ones"], writes=[(tag, "pstat")])
                S.op("act", (lambda e, o=rstd[s], i=ps_stat, w=w:
                             e.activation(out=o[:, :w], in_=i[:, :w], func=AF.Sqrt, bias=self.epsc[:, 0:1], scale=1.0 / D)),
                     reads=[(tag, "pstat"), "epsc"], writes=[(tag, "rstd", s)])
                S.op("dve", (lambda e, o=rstd[s], w=w: e.reciprocal(o[:, :w], o[:, :w])),
                     reads=[(tag, "rstd", s)], writes=[(tag, "rstd", s)])

        def apply(t0, gkey, src=src, widths=None, xn=xn, xkey="xn"):
            srcv = self.xview(src)
            if widths is None:
                widths = [TS] * ns
            offs = [sum(widths[:i]) for i in range(len(widths))]
            for s in range(len(widths)):
                ts0 = t0 + offs[s]
                w = widths[s]
                for kg in range(KD // XG):
                    b = cnt[0] % NXG
                    cnt[0] += 1
                    S.dma("sp", xg[b][:, :, :w], srcv[:, kg * XG:(kg + 1) * XG, ts0:ts0 + w],
                          writes=[(tag, "xg", b)], sem=f"{tag}xg{b}")
                    for kk in range(XG):
                        k = kg * XG + kk
                        S.op("dve", (lambda e, o=xn[s], i=xg[b], kk=kk, k=k, r=rstd[s], w=w:
                                     e.scalar_tensor_tensor(out=o[:, k, :w], in0=i[:, kk, :w],
                                                            scalar=self.vcol(gkey, k), in1=r[:, :w],
                                                            op0=ALU.mult, op1=ALU.mult)),
                             reads=[(tag, "xg", b), (tag, "rstd", s), "vec"], writes=[(tag, xkey, s, k)])

        def run(t0, gkey, src=src, widths=None, xn=xn, xkey="xn"):
            stats(t0, src=src, widths=widths)
            apply(t0, gkey, src=src, widths=widths, xn=xn, xkey=xkey)
        def gen(t0, gkey, xn=xn, xkey="xn", lag=3, src=src):
            srcv = self.xview(src)
            pending = []

            def push(fn):
                pending.append(fn)
                if len(pending) > lag:
                    pending.pop(0)()

            for s in range(ns):
                ts0 = t0 + s * TS
                for kg in range(KD // XG):
                    b = cnt[0] % NXG
                    cnt[0] += 1
                    S.dma("sp", xg[b][:], srcv[:, kg * XG:(kg + 1) * XG, ts0:ts0 + TS],
                          writes=[(tag, "xg", b)], sem=f"{tag}xg{b}")

                    def comp(b=b, kg=kg, s=s):
                        for kk in range(XG):
                            k = kg * XG + kk
                            q = sqc[0] % NSQ
                            sqc[0] += 1
                            S.op("act", (lambda e, o=sq[q], i=xg[b], kk=kk:
                                         e.activation(out=o[:], in_=i[:, kk, :], func=AF.Square)),
                                 reads=[(tag, "xg", b)], writes=[(tag, "sq", q)])
                            S.op("pe", (lambda e, o=ps_stat, i=sq[q], k=k:
                                        e.matmul(o[:], self.ones[:], i[:], start=(k == 0), stop=(k == KD - 1))),
                                 reads=[(tag, "sq", q), "ones"], writes=[(tag, "pstat")])
                        if kg == KD // XG - 1:
                            S.op("act", (lambda e, o=rstd[s], i=ps_stat:
                                         e.activation(out=o[:], in_=i[:], func=AF.Sqrt, bias=self.epsc[:, 0:1], scale=1.0 / D)),
                                 reads=[(tag, "pstat"), "epsc"], writes=[(tag, "rstd", s)])
                            S.op("dve", (lambda e, o=rstd[s]: e.reciprocal(o[:], o[:])),
                                 reads=[(tag, "rstd", s)], writes=[(tag, "rstd", s)])
                    push(comp)
                    yield
            for s in range(ns):
                ts0 = t0 + s * TS
                for kg in range(KD // XG):
                    b = cnt[0] % NXG
                    cnt[0] += 1
                    S.dma("sp", xg[b][:], srcv[:, kg * XG:(kg + 1) * XG, ts0:ts0 + TS],
                          writes=[(tag, "xg", b)], sem=f"{tag}xg{b}")

                    def comp2(b=b, kg=kg, s=s):
                        for kk in range(XG):
                            k = kg * XG + kk
                            S.op("dve", (lambda e, o=xn[s], i=xg[b], kk=kk, k=k, r=rstd[s]:
                                         e.scalar_tensor_tensor(out=o[:, k, :], in0=i[:, kk, :],
                                                                scalar=self.vcol(gkey, k), in1=r[:],
                                                                op0=ALU.mult, op1=ALU.mult)),
                                 reads=[(tag, "xg", b), (tag, "rstd", s), "vec"], writes=[(tag, xkey, s, k)])
                    push(comp2)
                    yield
            for _ in range(lag):
                yield
                if pending:
                    pending.pop(0)()
            while pending:
                pending.pop(0)()

        run.stats = stats
        run.apply = apply
        run.gen = gen
        cnt = [0]
        sqc = [0]
        return run

    def wload(self, wbuf, slot, src_ap, key, sem):
        self.S.dma("pool", wbuf[slot][:].rearrange("p k c -> p (k c)"), src_ap,
                   writes=[(key, slot)], sem=f"{sem}{slot}")

    @staticmethod
    def ffn_jobs(lo, hi, unit=64):
        n = (hi - lo) // unit
        assert (hi - lo) % unit == 0
        per = 2 * TS // unit
        njobs = -(-n // per)
        base, extra = divmod(n, njobs)
        jobs = []
        t = lo
        for i in range(njobs):
            u = base + (1 if i < extra else 0)
            u0 = (u + 1) // 2
            w = [u0 * unit, (u - u0) * unit]
            w = [x for x in w if x > 0]
            jobs.append((t, w))
            t += u * unit
        assert t == hi
        return jobs

    def ph_ffn(self, l, f, rng, src):
        nc, S = self.nc, self.S
        lf = l * 2 + f
        lo, hi = rng
        with ExitStack() as st:
            xn = [st.enter_context(nc.sbuf_tensor(f"f_xn{s}", [P, KD, TS], BF16)) for s in range(2)]
            h = [st.enter_context(nc.sbuf_tensor(f"f_h{s}", [P, KF, TS], BF16)) for s in range(2)]
            NWG = 3
            wgb = [st.enter_context(nc.sbuf_tensor(f"f_wg{i}", [P, KD, P], BF16)) for i in range(NWG)]
            wub = [st.enter_context(nc.sbuf_tensor(f"f_wu{i}", [P, KD, P], BF16)) for i in range(NWG)]
            wdb = [st.enter_context(nc.sbuf_tensor(f"f_wd{i}", [P, KF, P], BF16)) for i in range(2)]
            sg = [st.enter_context(nc.sbuf_tensor(f"f_sg{i}", [P, TS], BF16)) for i in range(2)]
            xr = [st.enter_context(nc.sbuf_tensor(f"f_xr{i}", [P, TS], F32)) for i in range(2)]
            yo = [st.enter_context(nc.sbuf_tensor(f"f_yo{i}", [P, TS], F32)) for i in range(2)]
            ps_stat = st.enter_context(nc.psum_tensor("f_pstat", [P, TS], F32))
            pg = [st.enter_context(nc.psum_tensor(f"f_pg{i}", [P, TS], F32)) for i in range(2)]
            pu = [st.enter_context(nc.psum_tensor(f"f_pu{i}", [P, TS], F32)) for i in range(2)]
            pd = [st.enter_context(nc.psum_tensor(f"f_pd{i}", [P, TS], F32)) for i in range(2)]
            norm = self.emit_norm(st, src, 0, 2, None, xn, ps_stat, "fn")
            srcv = self.xview(src)
            dstv = self.xview(self.xs)
            wcnt = 0
            dcnt = 0
            rcnt = 0
            jobs = self.ffn_jobs(lo, hi)
            norm(jobs[0][0], ("ffn_norm", l, f), widths=jobs[0][1])
            for ji, (t0, widths) in enumerate(jobs):
                offs = [sum(widths[:i]) for i in range(len(widths))]
                ns = len(widths)
                nxt = jobs[ji + 1] if ji + 1 < len(jobs) else None
                for j in range(KF):
                    if j == KF - 12 and nxt is not None:
                        norm.stats(nxt[0], widths=nxt[1])
                    slot = wcnt % NWG
                    wcnt += 1
                    self.wload(wgb, slot, self.wg[lf, j], "wg", "fwg")
                    self.wload(wub, slot, self.wu[lf, j], "wu", "fwu")
                    for s in range(ns):
                        w = widths[s]
                        for k in range(KD):
                            S.op("pe", (lambda e, o=pg[s], w_=wgb[slot], x=xn[s], k=k, w=w:
                                        e.matmul(o[:, :w], w_[:, k, :], x[:, k, :w], start=(k == 0), stop=(k == KD - 1))),
                                 reads=[("wg", slot), ("fn", "xn", s, k)], writes=[("pg", s)])
                        for k in range(KD):
                            S.op("pe", (lambda e, o=pu[s], w_=wub[slot], x=xn[s], k=k, w=w:
                                        e.matmul(o[:, :w], w_[:, k, :], x[:, k, :w], start=(k == 0), stop=(k == KD - 1))),
                                 reads=[("wu", slot), ("fn", "xn", s, k)], writes=[("pu", s)])
                        S.op("act", (lambda e, o=sg[s], i=pg[s], w=w:
                                     e.activation(out=o[:, :w], in_=i[:, :w], func=AF.Silu)),
                             reads=[("pg", s)], writes=[("sg", s)])
                        S.op("dve", (lambda e, o=h[s], a_=sg[s], b_=pu[s], j=j, w=w:
                                     e.tensor_tensor(o[:, j, :w], a_[:, :w], b_[:, :w], ALU.mult)),
                             reads=[("sg", s), ("pu", s)], writes=[("h", s, j)])
                if nxt is not None:
                    norm.apply(nxt[0], ("ffn_norm", l, f), widths=nxt[1])
                for o_ in range(KD):
                    slot = dcnt % 2
                    dcnt += 1
                    self.wload(wdb, slot, self.wd[lf, o_], "wd", "fwd")
                    for s in range(ns):
                        w = widths[s]
                        ts0 = t0 + offs[s]
                        rb = rcnt % 2
                        rcnt += 1
                        S.dma("sp", xr[rb][:, :w], srcv[:, o_, ts0:ts0 + w], writes=[("xr", rb)], sem=f"fxr{rb}")
                        for j in range(KF):
                            S.op("pe", (lambda e, o=pd[s], w_=wdb[slot], x=h[s], j=j, w=w:
                                        e.matmul(o[:, :w], w_[:, j, :], x[:, j, :w], start=(j == 0), stop=(j == KF - 1))),
                                 reads=[("wd", slot), ("h", s, j)], writes=[("pd", s)])
                        S.op("dve", (lambda e, o=yo[rb], i=pd[s], x=xr[rb], w=w:
                                     e.scalar_tensor_tensor(out=o[:, :w], in0=i[:, :w], scalar=0.5, in1=x[:, :w],
                                                            op0=ALU.mult, op1=ALU.add)),
                             reads=[("pd", s), ("xr", rb)], writes=[("yo", rb)])
                        S.dma("sp", dstv[:, o_, ts0:ts0 + w], yo[rb][:, :w], reads=[("yo", rb)], sem=f"fyo{rb}")
            S.emit()

    def ph_qkv(self, l, j, rng):
        nc, S = self.nc, self.S
        a, b = rng
        assert (b - a) % 2 == 0
        with ExitStack() as st:
            xns = [[st.enter_context(nc.sbuf_tensor(f"q_xn{b_}{s}", [P, KD, TS], BF16)) for s in range(2)] for b_ in range(2)]
            xn = xns[0]
            NW = 3
            wb = [st.enter_context(nc.sbuf_tensor(f"q_w{i}", [P, KD, P], BF16)) for i in range(NW)]
            wvb = [st.enter_context(nc.sbuf_tensor(f"q_wv{i}", [P, KD, 512], BF16)) for i in range(2)]
            cs = st.enter_context(nc.sbuf_tensor("q_cos", [32, 2 * TS], F32))
            sn = st.enter_context(nc.sbuf_tensor("q_sin", [32, 2 * TS], F32))
            NQ = 4
            qb = [st.enter_context(nc.sbuf_tensor(f"q_qb{i}", [P, TS], BF16)) for i in range(NQ)]
            t1 = [st.enter_context(nc.sbuf_tensor(f"q_t1{i}", [32, TS], F32)) for i in range(3)]
            t2 = [st.enter_context(nc.sbuf_tensor(f"q_t2{i}", [32, TS], F32)) for i in range(3)]
            pend = [None]
            vb = [st.enter_context(nc.sbuf_tensor(f"q_vb{i}", [P, 512], BF16)) for i in range(3)]
            ps_stat = st.enter_context(nc.psum_tensor("q_pstat", [P, TS], F32))
            pq = [st.enter_context(nc.psum_tensor(f"q_pq{i}", [P, TS], F32)) for i in range(2)]
            pp = [st.enter_context(nc.psum_tensor(f"q_pp{i}", [P, TS], F32)) for i in range(2)]
            pv = [st.enter_context(nc.psum_tensor(f"q_pv{i}", [P, 512], F32)) for i in range(2)]
            norm = self.emit_norm(st, self.xs, 0, 2, None, xn, ps_stat, "qn", NXG=6, NSQ=4)
            wcnt = 0
            qcnt = 0
            vcnt = 0
            wvcnt = 0
            npair = (b - a) // 2
            norm(a * TS, ("mix_norm", l), xn=xns[0], xkey="xn0")
            for pi in range(npair):
                t0 = (a + 2 * pi) * TS
                xn = xns[pi % 2]
                xk = f"xn{pi % 2}"
                S.dma("sp", cs[:], self.cosT[:, t0:t0 + 2 * TS], writes=["cos"], sem="qcs")
                S.dma("sp", sn[:], self.sinT[:, t0:t0 + 2 * TS], writes=["sin"], sem="qcs")
                ngen = None
                if pi + 1 < npair:
                    ngen = norm.gen(t0 + 2 * TS, ("mix_norm", l), xn=xns[(pi + 1) % 2], xkey=f"xn{(pi + 1) % 2}")
                for c in range(24):
                    if ngen is not None:
                        for _ in range(2):
                            next(ngen, None)
                    slot = wcnt % NW
                    wcnt += 1
                    self.wload(wb, slot, self.wqk[j, c], "w", "qw")
                    dst = self.qs if c < 12 else self.ks
                    hd = c % 12
                    for s in range(2):
                        ts0 = t0 + s * TS
                        for k in range(KD):
                            S.op("pe", (lambda e, o=pq[s], w=wb[slot], x=xn[s], k=k:
                                        e.matmul(o[:], w[:, k, :], x[:, k, :], start=(k == 0), stop=(k == KD - 1))),
                                 reads=[("w", slot), ("qn", xk, s, k)], writes=[("pq", s)])
                        qi = qcnt % NQ
                        ti = qcnt % 3
                        qcnt += 1
                        S.op("act", (lambda e, o=qb[qi], i=pq[s]: e.copy(o[:], i[:])),
                             reads=[("pq", s)], writes=[("qb", qi)])
                        S.op("dve", (lambda e, o=t2[ti], i=pq[s], s=s:
                                     e.tensor_tensor(o[:], i[0:32, :], cs[:, s * TS:(s + 1) * TS], ALU.mult)),
                             reads=[("pq", s), "cos", ("qb", qi)], writes=[("t2", ti)])
                        if pend[0] is not None:
                            pend[0]()

                        def fin(qi=qi, ti=ti, s=s, ts0=ts0, dst=dst, hd=hd, pb_=qcnt % 2):
                            S.op("pe", (lambda e, o=pp[pb_], i=qb[qi]:
                                        e.matmul(o[0:32, :], self.perm[0:32, 0:32], i[0:32, :], start=True, stop=True)),
                                 reads=[("qb", qi), "perm"], writes=[("pp", pb_)])
                            S.op("dve", (lambda e, o=t1[ti], i=pp[pb_], s=s:
                                         e.tensor_tensor(o[:], i[0:32, :], sn[:, s * TS:(s + 1) * TS], ALU.mult)),
                                 reads=[("pp", pb_), "sin"], writes=[("t1", ti)])
                            S.op("dve", (lambda e, o=qb[qi], x=t1[ti], y=t2[ti]:
                                         e.tensor_tensor(o[0:32, :], x[:], y[:], ALU.add)),
                                 reads=[("t1", ti), ("t2", ti), ("qb", qi)], writes=[("qb", qi)])
                            S.dma("sp", dst[hd * P:(hd + 1) * P, ts0:ts0 + TS], qb[qi][:], reads=[("qb", qi)], sem=f"qst{qi}")
                        pend[0] = fin
                if pend[0] is not None:
                    pend[0]()
                    pend[0] = None
                if ngen is not None:
                    for _ in ngen:
                        pass
                for g in range(3):
                    slot = wvcnt % 2
                    wvcnt += 1
                    S.dma("pool", wvb[slot][:].rearrange("p k c -> p (k c)"), self.wv[j, g],
                          writes=[("wv", slot)], sem=f"qwv{slot}")
                    for s in range(2):
                        for tb in range(4):
                            pb = vcnt % 2
                            vi = vcnt % 3
                            vcnt += 1
                            for k in range(KD):
                                S.op("pe", (lambda e, o=pv[pb], w=wvb[slot], x=xn[s], k=k, tb=tb:
                                            e.matmul(o[:], x[:, k, tb * P:(tb + 1) * P], w[:, k, :],
                                                     start=(k == 0), stop=(k == KD - 1))),
                                     reads=[("wv", slot), ("qn", xk, s, k)], writes=[("pv", pb)])
                            S.op("act", (lambda e, o=vb[vi], i=pv[pb]: e.copy(o[:], i[:])),
                                 reads=[("pv", pb)], writes=[("vb", vi)])
                            r0 = t0 + s * TS + tb * P
                            S.dma("sp", self.vs[r0:r0 + P, g * 512:(g + 1) * 512], vb[vi][:],
                                  reads=[("vb", vi)], sem=f"qvs{vi}")
            S.emit()

    def ph_att(self, rng):
        nc, S = self.nc, self.S
        scale = 1.0 / math.sqrt(128.0)
        AT = 4 * TS

        def sl(base, n, step):
            return slice(base, base + (n - 1) * step + 1, step)

        with ExitStack() as st:
            qt = [st.enter_context(nc.sbuf_tensor(f"a_qt{i}", [P, AT], BF16)) for i in range(2)]
            kt = [st.enter_context(nc.sbuf_tensor(f"a_kt{i}", [P, 2 * AT], BF16)) for i in range(2)]
            vt = [st.enter_context(nc.sbuf_tensor(f"a_vt{i}", [P, 32, P], BF16)) for i in range(2)]
            oq = st.enter_context(nc.sbuf_tensor("a_oq", [3, AT], BF16))
            ok_ = st.enter_context(nc.sbuf_tensor("a_ok", [3, 2 * AT], BF16))
            nd = st.enter_context(nc.sbuf_tensor("a_nd", [P, 3, 2, AT], F32))
            dt_ = st.enter_context(nc.sbuf_tensor("a_dt", [P, AT], F32))
            ob = [st.enter_context(nc.sbuf_tensor(f"a_ob{i}", [P, AT], BF16)) for i in range(2)]
            pt = [st.enter_context(nc.sbuf_tensor(f"a_pt{i}", [P, 256], BF16)) for i in range(3)]
            negb = st.enter_context(nc.sbuf_tensor("a_negb", [P, 1], F32))
            ps_s = [st.enter_context(nc.psum_tensor(f"a_ps{i}", [P, 256], F32)) for i in range(3)]
            ps_n = [st.enter_context(nc.psum_tensor(f"a_pn{i}", [P, 2, P], F32)) for i in range(3)]
            S.op("pool", (lambda e: e.memset(negb[:], -BIG * scale)), writes=["negb"])
            pend = [None]

            def pv_part(pi_, hb, r, m, nb, g, qc):
                bA = r * nb + m
                S.op("pe", (lambda e, o=ps_n[pi_], v=vt[hb], p_=pt[pi_], bA=bA:
                            e.matmul(o[:, 0, :], v[:, bA, :], p_[:, 0:P], start=True, stop=False)),
                     reads=[("vt", hb, r), ("pt", pi_)], writes=[("psn", pi_)])
                S.op("pe", (lambda e, o=ps_n[pi_], v=vt[hb], p_=pt[pi_], bA=bA:
                            e.matmul(o[:, 0, :], v[:, bA + 1, :], p_[:, P:2 * P], start=False, stop=True)),
                     reads=[("vt", hb, r), ("pt", pi_)], writes=[("psn", pi_)])
                S.op("pe", (lambda e, o=ps_n[pi_], p_=pt[pi_]:
                            e.matmul(o[:, 1, :], self.ones[:], p_[:, 0:P], start=True, stop=False)),
                     reads=["ones", ("pt", pi_)], writes=[("psn", pi_)])
                S.op("pe", (lambda e, o=ps_n[pi_], p_=pt[pi_]:
                            e.matmul(o[:, 1, :], self.ones[:], p_[:, P:2 * P], start=False, stop=True)),
                     reads=["ones", ("pt", pi_)], writes=[("psn", pi_)])
                S.op("dve", (lambda e, i=ps_n[pi_], g=g, qc=qc:
                             e.tensor_copy(nd[:, g, :, qc], i[:])),
                     reads=[("psn", pi_)], writes=[("nd", g)])

            hcnt = 0
            qbc = 0
            ocnt = 0
            for a0 in att_tiles(rng):
                T0 = a0 * TS
                S.dma("pool", oq[:], self.ohq[:, T0:T0 + AT], writes=["oq"], sem="aoq")
                S.dma("pool", ok_[:], self.ohk[:, T0 - 1024:T0 + AT + 1024], writes=["ok"], sem="aoq")
                for h in range(4):
                    for g in range(3):
                        hd = g * 4 + h
                        d = DIL[g]
                        halo = 64 * d
                        nqb = 16 // d
                        nb = nqb + 1
                        hb = hcnt % 2
                        hcnt += 1
                        S.dma("sp", qt[hb][:], self.qs[hd * P:(hd + 1) * P, T0:T0 + AT], writes=[("qt", hb)], sem=f"aq{hb}")
                        S.dma("sp", kt[hb][:, 0:AT + 2 * halo], self.ks[hd * P:(hd + 1) * P, T0 - halo:T0 + AT + halo],
                              writes=[("kt", hb)], sem=f"ak{hb}")
                        for r in range(d):
                            start = T0 - halo + r
                            src = self.vs[sl(start, P * nb, d), hd * P:(hd + 1) * P].rearrange("(b j) c -> j b c", j=P)
                            S.dma("sp", vt[hb][:, r * nb:(r + 1) * nb, :], src, writes=[("vt", hb, r)], sem=f"av{hb}")
                        for r in range(d):
                            for m in range(nqb):
                                qc = sl(P * m * d + r, P, d)
                                kA = sl(P * m * d + r, P, d)
                                kB = sl(P * (m + 1) * d + r, P, d)
                                off = 1024 - halo
                                oA = sl(P * m * d + r + off, P, d)
                                oB = sl(P * (m + 1) * d + r + off, P, d)
                                pi_ = qbc % 3
                                qbc += 1
                                for half, kc, oc in ((0, kA, oA), (1, kB, oB)):
                                    o_ap = (lambda half=half, pi_=pi_: ps_s[pi_][:, half * P:(half + 1) * P])
                                    S.op("pe", (lambda e, oa=o_ap, kc=kc, qc=qc, hb=hb:
                                                e.matmul(oa(), kt[hb][:, kc], qt[hb][:, qc], start=True, stop=False)),
                                         reads=[("kt", hb), ("qt", hb)], writes=[("pss", pi_)])
                                    S.op("pe", (lambda e, oa=o_ap, oc=oc, qc=qc:
                                                e.matmul(oa(), ok_[0:3, oc], oq[0:3, qc], start=False, stop=False)),
                                         reads=["ok", "oq"], writes=[("pss", pi_)])
                                    S.op("pe", (lambda e, oa=o_ap, half=half:
                                                e.matmul(oa(), self.ident[:], self.maskb[:, half * P:(half + 1) * P],
                                                         start=False, stop=True)),
                                         reads=["ident", "maskb"], writes=[("pss", pi_)])
                                S.op("act", (lambda e, o=pt[pi_], i=ps_s[pi_]:
                                             e.activation(out=o[:], in_=i[:], func=AF.Exp, bias=negb[:, 0:1], scale=scale)),
                                     reads=[("pss", pi_), "negb"], writes=[("pt", pi_)])
                                if pend[0] is not None:
                                    pend[0]()
                                pend[0] = (lambda pi_=pi_, hb=hb, r=r, m=m, nb=nb, g=g, qc=qc: pv_part(pi_, hb, r, m, nb, g, qc))
                    if pend[0] is not None:
                        pend[0]()
                        pend[0] = None
                    S.op("dve", (lambda e: e.tensor_tensor(dt_[:], nd[:, 0, 1, :], nd[:, 1, 1, :], ALU.add)),
                         reads=[("nd", 0), ("nd", 1)], writes=["dt"])
                    S.op("pool", (lambda e: e.tensor_tensor(dt_[:], dt_[:], nd[:, 2, 1, :], ALU.add)),
                         reads=["dt", ("nd", 2)], writes=["dt"])
                    S.op("dve", (lambda e: e.reciprocal(dt_[:], dt_[:])), reads=["dt"], writes=["dt"])
                    for g in range(3):
                        hd = g * 4 + h
                        oi = ocnt % 2
                        ocnt += 1
                        eng = "dve" if g != 1 else "pool"
                        S.op(eng, (lambda e, o=ob[oi], g=g: e.tensor_tensor(o[:], nd[:, g, 0, :], dt_[:], ALU.mult)),
                             reads=[("nd", g), "dt"], writes=[("ob", oi)])
                        S.dma("sp", self.os_[hd * P:(hd + 1) * P, T0:T0 + AT], ob[oi][:], reads=[("ob", oi)], sem=f"ao{oi}")
            S.emit()

    def emit_proj_residual(self, st, tag, in_tiles, nk, w_dram, bias_key, t0, state):
        nc, S = self.nc, self.S
        if "wb" not in state:
            state["wb"] = [st.enter_context(nc.sbuf_tensor(f"{tag}_w{i}", [P, nk, P], BF16)) for i in range(3)]
            state["xr"] = [st.enter_context(nc.sbuf_tensor(f"{tag}_xr{i}", [P, TS], F32)) for i in range(2)]
            state["yo"] = [st.enter_context(nc.sbuf_tensor(f"{tag}_yo{i}", [P, TS], F32)) for i in range(2)]
            state["pd"] = [st.enter_context(nc.psum_tensor(f"{tag}_pd{i}", [P, TS], F32)) for i in range(2)]
            state["wc"] = 0
            state["rc"] = 0
        wb, xr, yo, pd = state["wb"], state["xr"], state["yo"], state["pd"]
        xv = self.xview(self.xs)
        for o_ in range(KD):
            slot = state["wc"] % 3
            state["wc"] += 1
            self.wload(wb, slot, w_dram[o_], (tag, "w"), f"{tag}w")
            for s in range(len(in_tiles)):
                ts0 = t0 + s * TS
                rb = state["rc"] % 2
                state["rc"] += 1
                S.dma("sp", xr[rb][:], xv[:, o_, ts0:ts0 + TS], writes=[(tag, "xr", rb)], sem=f"{tag}xr{rb}")
                for k in range(nk):
                    S.op("pe", (lambda e, o=pd[s], w=wb[slot], x=in_tiles[s], k=k:
                                e.matmul(o[:], w[:, k, :], x[:, k, :], start=(k == 0), stop=(k == nk - 1))),
                         reads=[((tag, "w"), slot), (tag, "in", s, k)], writes=[(tag, "pd", s)])
                if bias_key is None:
                    S.op("dve", (lambda e, o=yo[rb], i=pd[s], x=xr[rb]:
                                 e.tensor_tensor(o[:], i[:], x[:], ALU.add)),
                         reads=[(tag, "pd", s), (tag, "xr", rb)], writes=[(tag, "yo", rb)])
                else:
                    S.op("dve", (lambda e, o=yo[rb], i=pd[s], x=xr[rb], o_=o_:
                                 e.scalar_tensor_tensor(out=o[:], in0=i[:], scalar=self.vcol(bias_key, o_), in1=x[:],
                                                        op0=ALU.add, op1=ALU.add)),
                         reads=[(tag, "pd", s), (tag, "xr", rb), "vec"], writes=[(tag, "yo", rb)])
                S.dma("sp", xv[:, o_, ts0:ts0 + TS], yo[rb][:], reads=[(tag, "yo", rb)], sem=f"{tag}yo{rb}")

    def ph_wo(self, j, rng):
        nc, S = self.nc, self.S
        a, b = rng
        assert (b - a) % 2 == 0
        with ExitStack() as st:
            ot = [st.enter_context(nc.sbuf_tensor(f"o_ot{s}", [P, NHEAD, TS], BF16)) for s in range(2)]
            ov = self.os_.rearrange("(k p) t -> p k t", p=P)
            state = {}
            for pi in range((b - a) // 2):
                t0 = (a + 2 * pi) * TS
                for s in range(2):
                    ts0 = t0 + s * TS
                    S.dma("sp", ot[s][:], ov[:, :, ts0:ts0 + TS],
                          writes=[("wo", "in", s, k) for k in range(NHEAD)], sem=f"oot{s}")
                self.emit_proj_residual(st, "wo", ot, NHEAD, self.wo[j], None, t0, state)
            S.emit()

    def ph_c1(self, l, j, rng):
        nc, S = self.nc, self.S
        a, b = rng
        assert (b - a) % 2 == 0
        with ExitStack() as st:
            xns = [[st.enter_context(nc.sbuf_tensor(f"c_xn{b_}{s}", [P, KD, TS], BF16)) for s in range(2)] for b_ in range(2)]
            xn = xns[0]
            NW = 3
            wa = [st.enter_context(nc.sbuf_tensor(f"c_wa{i}", [P, KD, P], BF16)) for i in range(NW)]
            wb_ = [st.enter_context(nc.sbuf_tensor(f"c_wb{i}", [P, KD, P], BF16)) for i in range(NW)]
            sb = [st.enter_context(nc.sbuf_tensor(f"c_sb{i}", [P, TS], F32)) for i in range(2)]
            gl = [st.enter_context(nc.sbuf_tensor(f"c_gl{i}", [P, TS], BF16)) for i in range(3)]
            ps_stat = st.enter_context(nc.psum_tensor("c_pstat", [P, TS], F32))
            pa = [st.enter_context(nc.psum_tensor(f"c_pa{i}", [P, TS], F32)) for i in range(2)]
            pb = [st.enter_context(nc.psum_tensor(f"c_pb{i}", [P, TS], F32)) for i in range(2)]
            norm = self.emit_norm(st, self.xs, 0, 2, None, xn, ps_stat, "cn", NXG=6, NSQ=4)
            gv = self.xview(self.gs)
            wcnt = 0
            gcnt = 0
            npair = (b - a) // 2
            norm(a * TS, ("mix_norm", l), xn=xns[0], xkey="xn0")
            for pi in range(npair):
                t0 = (a + 2 * pi) * TS
                xn = xns[pi % 2]
                xk = f"xn{pi % 2}"
                ngen = None
                if pi + 1 < npair:
                    ngen = norm.gen(t0 + 2 * TS, ("mix_norm", l), xn=xns[(pi + 1) % 2], xkey=f"xn{(pi + 1) % 2}")
                for cc in range(KD):
                    if ngen is not None:
                        for _ in range(3):
                            next(ngen, None)
                    slot = wcnt % NW
                    wcnt += 1
                    self.wload(wa, slot, self.win[j, cc], "wa", "cwa")
                    self.wload(wb_, slot, self.win[j, KD + cc], "wb", "cwb")
                    for s in range(2):
                        ts0 = t0 + s * TS
                        for k in range(KD):
                            S.op("pe", (lambda e, o=pa[s], w=wa[slot], x=xn[s], k=k:
                                        e.matmul(o[:], w[:, k, :], x[:, k, :], start=(k == 0), stop=(k == KD - 1))),
                                 reads=[("wa", slot), ("cn", xk, s, k)], writes=[("pa", s)])
                        for k in range(KD):
                            S.op("pe", (lambda e, o=pb[s], w=wb_[slot], x=xn[s], k=k:
                                        e.matmul(o[:], w[:, k, :], x[:, k, :], start=(k == 0), stop=(k == KD - 1))),
                                 reads=[("wb", slot), ("cn", xk, s, k)], writes=[("pb", s)])
                        gi = gcnt % 3
                        gcnt += 1
                        S.op("act", (lambda e, o=sb[s], i=pb[s], cc=cc:
                                     e.activation(out=o[:], in_=i[:], func=AF.Sigmoid,
                                                  bias=self.vcol(("b_in", j), KD + cc), scale=1.0)),
                             reads=[("pb", s), "vec"], writes=[("sb", s)])
                        S.op("dve", (lambda e, o=gl[gi], i=pa[s], g_=sb[s], cc=cc:
                                     e.scalar_tensor_tensor(out=o[:], in0=i[:], scalar=self.vcol(("b_in", j), cc),
                                                            in1=g_[:], op0=ALU.add, op1=ALU.mult)),
                             reads=[("pa", s), ("sb", s), "vec"], writes=[("gl", gi)])
                        S.dma("sp", gv[:, cc, ts0:ts0 + TS], gl[gi][:], reads=[("gl", gi)], sem=f"cgl{gi}")
                if ngen is not None:
                    for _ in ngen:
                        pass
            S.emit()

    def ph_c2(self, j, rng):
        nc, S = self.nc, self.S
        a, b = rng
        assert (b - a) % 2 == 0
        GW = TS + CW - 1
        with ExitStack() as st:
            hcv = [st.enter_context(nc.sbuf_tensor(f"d_hcv{s}", [P, KD, TS], F32)) for s in range(2)]
            hn = [st.enter_context(nc.sbuf_tensor(f"d_hn{s}", [P, KD, TS], BF16)) for s in range(2)]
            gt = [st.enter_context(nc.sbuf_tensor(f"d_gt{i}", [P, GW], BF16)) for i in range(4)]
            dg = [st.enter_context(nc.sbuf_tensor(f"d_dg{i}", [P, CW, P], BF16)) for i in range(3)]
            hb = [st.enter_context(nc.sbuf_tensor(f"d_hb{i}", [P, TS], BF16)) for i in range(2)]
            sq = [st.enter_context(nc.sbuf_tensor(f"d_sq{i}", [P, TS], BF16)) for i in range(2)]
            mean = st.enter_context(nc.sbuf_tensor("d_mean", [P, TS], F32))
            msq = st.enter_context(nc.sbuf_tensor("d_msq", [P, TS], F32))
            rstd = st.enter_context(nc.sbuf_tensor("d_rstd", [P, TS], F32))
            ps_sum = [st.enter_context(nc.psum_tensor(f"d_psum{i}", [P, TS], F32)) for i in range(2)]
            ps_sq = [st.enter_context(nc.psum_tensor(f"d_psq{i}", [P, TS], F32)) for i in range(2)]
            pc = [st.enter_context(nc.psum_tensor(f"d_pc{i}", [P, TS], F32)) for i in range(2)]
            gv = self.xview(self.gs)
            state = {}
            gcnt = 0
            dcnt = 0
            wbase = self.VEC[("w_dw", j)]
            for pi in range((b - a) // 2):
                t0 = (a + 2 * pi) * TS
                for cc in range(KD):
                    di = dcnt % 3
                    dcnt += 1
                    en = "dve"
                    for k in range(CW):
                        c0 = wbase + cc * CW + k
                        S.op(en, (lambda e, o=dg[di], k=k, c0=c0:
                                  e.tensor_scalar(o[:, k, :], self.ident[:], self.vec_sb[:, c0:c0 + 1], None, ALU.mult)),
                             reads=["ident", "vec"], writes=[("dg", di)])
                    for s in range(2):
                        t = a + 2 * pi + s
                        ts0 = t * TS
                        gi = gcnt % 4
                        gcnt += 1
                        S.dma("sp", gt[gi][:], gv[:, cc, ts0 - 15:ts0 - 15 + GW], writes=[("gt", gi)], sem=f"dgt{gi}")
                        S.op("pool", (lambda e, g_=gt[gi], t=t:
                                      e.tensor_scalar(g_[:, 0:15], g_[:, 0:15], self.flag_sb[:, 2 * t:2 * t + 1], None, ALU.mult)),
                             reads=[("gt", gi), "flag"], writes=[("gt", gi)])
                        S.op("pool", (lambda e, g_=gt[gi], t=t:
                                      e.tensor_scalar(g_[:, TS + 15:GW], g_[:, TS + 15:GW],
                                                      self.flag_sb[:, 2 * t + 1:2 * t + 2], None, ALU.mult)),
                             reads=[("gt", gi), "flag"], writes=[("gt", gi)])
                        for k in range(CW):
                            S.op("pe", (lambda e, o=pc[s], w=dg[di], g_=gt[gi], k=k:
                                        e.matmul(o[:], w[:, k, :], g_[:, k:k + TS], start=(k == 0), stop=(k == CW - 1))),
                                 reads=[("dg", di), ("gt", gi)], writes=[("pc", s)])
                        S.op("act", (lambda e, o=hcv[s], i=pc[s], cc=cc:
                                     e.activation(out=o[:, cc, :], in_=i[:], func=AF.Identity,
                                                  bias=self.vcol(("b_dw", j), cc), scale=1.0)),
                             reads=[("pc", s), "vec"], writes=[("hcv", s, cc)])
                        q = gcnt % 2
                        S.op("act", (lambda e, o=hb[q], i=hcv[s], cc=cc: e.copy(o[:], i[:, cc, :])),
                             reads=[("hcv", s, cc)], writes=[("hb", q)])
                        S.op("pe", (lambda e, o=ps_sum[s], i=hb[q], cc=cc:
                                    e.matmul(o[:], self.ones[:], i[:], start=(cc == 0), stop=(cc == KD - 1))),
                             reads=[("hb", q), "ones"], writes=[("psum", s)])
                        S.op("act", (lambda e, o=sq[q], i=hcv[s], cc=cc:
                                     e.activation(out=o[:], in_=i[:, cc, :], func=AF.Square)),
                             reads=[("hcv", s, cc)], writes=[("sq", q)])
                        S.op("pe", (lambda e, o=ps_sq[s], i=sq[q], cc=cc:
                                    e.matmul(o[:], self.ones[:], i[:], start=(cc == 0), stop=(cc == KD - 1))),
                             reads=[("sq", q), "ones"], writes=[("psq", s)])
                for s in range(2):
                    S.op("dve", (lambda e, s=s: e.tensor_scalar(mean[:], ps_sum[s][:], 1.0 / D, None, ALU.mult)),
                         reads=[("psum", s)], writes=["mean"])
                    S.op("dve", (lambda e: e.tensor_tensor(msq[:], mean[:], mean[:], ALU.mult)),
                         reads=["mean"], writes=["msq"])
                    S.op("dve", (lambda e, s=s: e.scalar_tensor_tensor(out=msq[:], in0=ps_sq[s][:], scalar=1.0 / D, in1=msq[:],
                                                                       op0=ALU.mult, op1=ALU.subtract)),
                         reads=[("psq", s), "msq"], writes=["msq"])
                    S.op("act", (lambda e: e.activation(out=rstd[:], in_=msq[:], func=AF.Sqrt,
                                                        bias=self.epsc[:, 1:2], scale=1.0)),
                         reads=["msq", "epsc"], writes=["rstd"])
                    S.op("dve", (lambda e: e.reciprocal(rstd[:], rstd[:])), reads=["rstd"], writes=["rstd"])
                    for cc in range(KD):
                        S.op("dve", (lambda e, o=hcv[s], cc=cc: e.tensor_tensor(o[:, cc, :], o[:, cc, :], mean[:], ALU.subtract)),
                             reads=[("hcv", s, cc), "mean"], writes=[("hcv", s, cc)])
                        S.op("dve", (lambda e, o=hcv[s], cc=cc: e.tensor_tensor(o[:, cc, :], o[:, cc, :], rstd[:], ALU.mult)),
                             reads=[("hcv", s, cc), "rstd"], writes=[("hcv", s, cc)])
                        S.op("act", (lambda e, o=hn[s], i=hcv[s], cc=cc:
                                     e.activation(out=o[:, cc, :], in_=i[:, cc, :], func=AF.Silu,
                                                  bias=self.vcol(("ln_b", j), cc), scale=self.vcol(("ln_g", j), cc))),
                             reads=[("hcv", s, cc), "vec"], writes=[("c2", "in", s, cc)])
                self.emit_proj_residual(st, "c2", hn, KD, self.wout[j], ("b_out", j), t0, state)
            S.emit()

    def ph_copy(self, rng):
        nc, S = self.nc, self.S
        lo, hi = rng
        with ExitStack() as st:
            buf = [st.enter_context(nc.sbuf_tensor(f"cp{i}", [P, KD, TS], F32)) for i in range(2)]
            sv = self.xview(self.xin)
            dv = self.xview(self.xs)
            i = 0
            t = lo
            while t < hi:
                w = min(TS, hi - t)
                S.dma("sp", buf[i][:, :, :w], sv[:, :, t:t + w], writes=[("cp", i)], sem=f"cpl{i}")
                S.dma("sp", dv[:, :, t:t + w], buf[i][:, :, :w], reads=[("cp", i)], sem=f"cps{i}")
                i = 1 - i
                t += w
            S.emit()

    def ph_final(self, rng):
        nc, S = self.nc, self.S
        a, b = rng
        o0 = self.cfg.own[0]
        with ExitStack() as st:
            ps_stat = st.enter_context(nc.psum_tensor("z_pstat", [P, TS], F32))
            rstd = st.enter_context(nc.sbuf_tensor("z_rstd", [P, TS], F32))
            xg = [st.enter_context(nc.sbuf_tensor(f"z_xg{i}", [P, 2, TS], F32)) for i in range(4)]
            sq = [st.enter_context(nc.sbuf_tensor(f"z_sq{i}", [P, TS], BF16)) for i in range(2)]
            yo = [st.enter_context(nc.sbuf_tensor(f"z_yo{i}", [P, 2, TS], F32)) for i in range(3)]
            srcv = self.xview(self.xs)
            outv = self.xview(self.yout)
            cnt = 0
            ycnt = 0
            for t in range(a, b):
                ts0 = t * TS
                for kg in range(KD // 2):
                    bb = cnt % 4
                    cnt += 1
                    S.dma("sp", xg[bb][:], srcv[:, kg * 2:kg * 2 + 2, ts0:ts0 + TS], writes=[("xg", bb)], sem=f"zxg{bb}")
                    for kk in range(2):
                        k = kg * 2 + kk
                        q = k % 2
                        S.op("act", (lambda e, o=sq[q], i=xg[bb], kk=kk:
                                     e.activation(out=o[:], in_=i[:, kk, :], func=AF.Square)),
                             reads=[("xg", bb)], writes=[("sq", q)])
                        S.op("pe", (lambda e, i=sq[q], k=k:
                                    e.matmul(ps_stat[:], self.ones[:], i[:], start=(k == 0), stop=(k == KD - 1))),
                             reads=[("sq", q), "ones"], writes=["pstat"])
                S.op("act", (lambda e: e.activation(out=rstd[:], in_=ps_stat[:], func=AF.Sqrt,
                                                    bias=self.epsc[:, 0:1], scale=1.0 / D)),
                     reads=["pstat", "epsc"], writes=["rstd"])
                S.op("dve", (lambda e: e.reciprocal(rstd[:], rstd[:])),
                     reads=["rstd"], writes=["rstd"])
                for kg in range(KD // 2):
                    bb = cnt % 4
                    cnt += 1
                    S.dma("sp", xg[bb][:], srcv[:, kg * 2:kg * 2 + 2, ts0:ts0 + TS], writes=[("xg", bb)], sem=f"zxg{bb}")
                    yb = ycnt % 3
                    ycnt += 1
                    for kk in range(2):
                        k = kg * 2 + kk
                        S.op("dve", (lambda e, o=yo[yb], i=xg[bb], kk=kk, k=k:
                                     e.scalar_tensor_tensor(out=o[:, kk, :], in0=i[:, kk, :],
                                                            scalar=self.vcol(("final_norm",), k), in1=rstd[:],
                                                            op0=ALU.mult, op1=ALU.mult)),
                             reads=[("xg", bb), "rstd", "vec"], writes=[("yo", yb)])
                    oc = (t - o0) * TS
                    S.dma("sp", outv[:, kg * 2:kg * 2 + 2, oc:oc + TS], yo[yb][:], reads=[("yo", yb)], sem=f"zyo{yb}")
            S.emit()


def _bf16_exact(a):
    return a.astype(np.float32)


def host_consts():
    ident = np.eye(P, dtype=np.float32)
    ones = np.ones((P, P), np.float32)
    perm = np.zeros((P, P), np.float32)
    for i in range(16):
        perm[i + 16, i] = 1.0
        perm[i, i + 16] = 1.0
    mb = np.zeros((P, 256), np.float32)
    j = np.arange(P)[:, None]
    i = np.arange(P)[None, :]
    mb[:, 0:128] = np.where(j >= i, 0.0, -BIG)
    mb[:, 128:256] = np.where(j <= i, 0.0, -BIG)
    return np.concatenate([ident, ones, perm, mb], axis=1)


def pk(v):
    v = np.asarray(v, np.float32)
    return np.ascontiguousarray(v.reshape(-1, P).T)


def host_vecs(inp):
    n = Builder.n_vec_cols()
    V = Builder.VEC
    t = np.zeros((P, n), np.float32)

    def put(key, arr):
        t[:, V[key]:V[key] + arr.shape[1]] = arr
    for l in range(DEPTH):
        for f in range(2):
            put(("ffn_norm", l, f), pk(inp["ffn_norm"][l, f]))
        put(("mix_norm", l), pk(inp["mix_norm"][l]))
    for j in range(2):
        put(("b_in", j), pk(inp["conv_b_in"][j]))
        wdw = np.asarray(inp["conv_w_dw"][j], np.float32)
        put(("w_dw", j), np.ascontiguousarray(wdw.reshape(CW, KD, P).transpose(2, 1, 0)).reshape(P, KD * CW))
        put(("b_dw", j), pk(inp["conv_b_dw"][j]))
        put(("ln_g", j), pk(inp["conv_ln_g"][j]))
        put(("ln_b", j), pk(inp["conv_ln_b"][j]))
        put(("b_out", j), pk(inp["conv_b_out"][j]))
    put(("final_norm",), pk(inp["final_norm"]))
    return t


def relayout_w(w, nout_chunk=P):
    w = np.asarray(w, np.float32)
    K, N = w.shape
    a = w.reshape(K // P, P, N // nout_chunk, nout_chunk).transpose(2, 1, 0, 3)
    return np.ascontiguousarray(a).reshape(N // nout_chunk, P, (K // P) * nout_chunk)


def host_weights(inp):
    out = {}
    out["wg"] = np.stack([relayout_w(inp["ffn_w_gate"][l, f]) for l in range(DEPTH) for f in range(2)])
    out["wu"] = np.stack([relayout_w(inp["ffn_w_up"][l, f]) for l in range(DEPTH) for f in range(2)])
    out["wd"] = np.stack([relayout_w(inp["ffn_w_down"][l, f]) for l in range(DEPTH) for f in range(2)])
    wqkv = np.asarray(inp["attn_w_qkv"], np.float32)
    out["wqk"] = np.stack([relayout_w(wqkv[j][:, :2 * AW]) for j in range(2)])
    out["wv"] = np.stack([relayout_w(wqkv[j][:, 2 * AW:], 512) for j in range(2)])
    out["wo"] = np.stack([relayout_w(inp["attn_w_o"][j]) for j in range(2)])
    out["win"] = np.stack([relayout_w(inp["conv_w_in"][j]) for j in range(2)])
    out["wout"] = np.stack([relayout_w(inp["conv_w_out"][j]) for j in range(2)])
    return out


def core_tables(c, nt=NT, halo=HALO):
    ll = nt * TS
    g = OWN * c - halo + np.arange(ll)
    valid = (g >= 0) & (g < NSEQ * SEQ)
    seg = np.where(valid, g // SEQ, -1)
    pos = np.where(valid, g % SEQ, 0).astype(np.float32)
    freqs = (500000.0 ** (-(np.arange(0, 32, 2, dtype=np.float32) / np.float32(32.0)))).astype(np.float32)
    ang = (pos[None, :] * freqs[:, None]).astype(np.float32)
    cos = np.cos(ang).astype(np.float32)
    sin = np.sin(ang).astype(np.float32)
    cosT = np.concatenate([cos, cos], axis=0)
    sinT = np.concatenate([-sin, sin], axis=0)
    segs = [s for s in np.unique(seg) if s >= 0]
    cls = np.full(ll, 2)
    for i, s in enumerate(segs[:2]):
        cls[seg == s] = i
    assert len(segs) <= 2
    oh = np.zeros((3, ll), np.float32)
    oh[cls, np.arange(ll)] = 1.0
    tcls = cls.reshape(nt, TS)[:, 0]
    fl = np.zeros((nt, 2), np.float32)
    for t in range(nt):
        fl[t, 0] = 1.0 if (t > 0 and tcls[t - 1] == tcls[t]) else 0.0
        fl[t, 1] = 1.0 if (t < nt - 1 and tcls[t + 1] == tcls[t]) else 0.0
    flags = np.broadcast_to(fl.reshape(1, nt * 2), (P, nt * 2)).copy()
    return dict(cosT=cosT, sinT=sinT, ohk=oh, ohq=oh * BIG, flags=flags, g=g, valid=valid)


def core_x(xflat, c, nt=NT, halo=HALO):
    ll = nt * TS
    g0 = OWN * c - halo
    lo = max(g0, 0)
    hi = min(g0 + ll, xflat.shape[0])
    out = np.zeros((D, ll), np.float32)
    out[:, lo - g0:hi - g0] = xflat[lo:hi].T
    return out


_CACHE = {}


def kernel(**inputs):
    inp = {k: np.asarray(v) for k, v in inputs.items()}
    xflat = np.concatenate([inp["x_prompt"].reshape(-1, D), inp["x_sample"].reshape(-1, D)], axis=0)
    cfg = Cfg()
    nc = Builder(cfg).build()
    W = host_weights(inp)
    vecs = host_vecs(inp)
    consts = host_consts()
    in_maps = []
    for c in range(NCORES):
        tb = core_tables(c)
        m = dict(W)
        m.update(xin=core_x(xflat, c), vecs=vecs, consts=consts, cosT=tb["cosT"], sinT=tb["sinT"],
                 ohk=tb["ohk"], ohq=tb["ohq"], flags=tb["flags"])
        in_maps.append(m)
    res = run_bass_kernel_spmd(nc, in_maps, core_ids=list(range(NCORES)))
    y = np.concatenate([np.asarray(r["yT"]).T for r in res.results], axis=0)
    y = np.ascontiguousarray(y, dtype=np.float32)
    yp = y[:2 * SEQ].reshape(2, SEQ, D)
    ysm = y[2 * SEQ:].reshape(1, SEQ, D)
    return (yp, ysm)
```
